# Optimizing a Trainium2 kernel written in Bass

```python
import math
import jax
import jax.numpy as jnp
from jax import lax
import numpy as np

D_MODEL = 1024
BATCH = 8
SEQ = 4096
DEPTH = 4

CTX_LEN = 256
GRID_W = 64
HEAD_DIM = 64
DN_HEADS = 4
GA_HEADS = 4
GA_KV_HEADS = 2
WA_HEADS = 4
WA_KV_HEADS = 2
FT_GROUPS = 4
FT_GROUP_DIM = 64
D_MIX = (DN_HEADS + GA_HEADS + WA_HEADS) * HEAD_DIM + FT_GROUPS * FT_GROUP_DIM
DN_DIM = DN_HEADS * HEAD_DIM
DN_COLS = 4 * DN_DIM + 4 * DN_HEADS
GA_COLS = (GA_HEADS + 2 * GA_KV_HEADS) * HEAD_DIM
WA_COLS = (WA_HEADS + 2 * WA_KV_HEADS) * HEAD_DIM
FT_COLS = FT_GROUPS * FT_GROUP_DIM
IN_COLS = DN_COLS + GA_COLS + WA_COLS + FT_COLS
CONV_W = 3
DN_CHUNK = 64
Q_BLOCK = 128
WINDOW = 128
ROPE_THETA = 10000.0
FFN_HIDDEN = 256 * math.ceil(8 * D_MODEL / (3 * 256))
N_MOD = 6
EPS = 1e-6
NEG_INF = -1e30

kernel_name = 'hybrid_dit_deltanet_gqa_window_fourier'


def _split(t, sizes):
    return jnp.split(t, [int(s) for s in np.cumsum(sizes)[:-1]], axis=-1)


def rms_norm(t, gain):
    tf = t.astype(jnp.float32)
    y = tf * lax.rsqrt(jnp.mean(tf * tf, -1, keepdims=True) + EPS)
    return (y * gain.astype(jnp.float32)).astype(t.dtype)


def l2_normalize(t):
    return t * lax.rsqrt(jnp.sum(t * t, -1, keepdims=True) + EPS)


def modulate(h, shift, scale):
    return h * (1 + scale) + shift


def swiglu(h, w_gate, w_up, w_down):
    return (jax.nn.silu(h @ w_gate) * (h @ w_up)) @ w_down


def axial_rope(n_tokens):
    rows = n_tokens // GRID_W
    row = jnp.repeat(jnp.arange(rows, dtype=jnp.float32), GRID_W)
    col = jnp.tile(jnp.arange(GRID_W, dtype=jnp.float32), rows)
    n_freq = HEAD_DIM // 4
    inv_freq = ROPE_THETA ** (-jnp.arange(n_freq, dtype=jnp.float32) / n_freq)
    ang = jnp.concatenate([row[:, None] * inv_freq, col[:, None] * inv_freq], -1)
    return jnp.cos(ang), jnp.sin(ang)


def apply_rope(t, cos, sin):
    tf = t.astype(jnp.float32)
    t1, t2 = tf[..., :HEAD_DIM // 2], tf[..., HEAD_DIM // 2:]
    cs, sn = cos[:, None, :], sin[:, None, :]
    return jnp.concatenate([t1 * cs - t2 * sn, t1 * sn + t2 * cs], -1).astype(t.dtype)


def short_conv(t, w):
    r = CONV_W // 2
    n = t.shape[1]
    tp = jnp.pad(t, ((0, 0), (r, r), (0, 0)))
    out = tp[:, 0:n] * w[0]
    for j in range(1, CONV_W):
        out = out + tp[:, j:j + n] * w[j]
    return out


def deltanet_prep(p, conv_w, A_log, dt_bias):
    B, S, _ = p.shape
    qkv, z, a, b = _split(p, [3 * DN_DIM, DN_DIM, 2 * DN_HEADS, 2 * DN_HEADS])
    qkv = jax.nn.silu(short_conv(qkv, conv_w)).astype(jnp.float32)
    q, k, v = [t.reshape(B, S, DN_HEADS, HEAD_DIM) for t in jnp.split(qkv, 3, -1)]
    q = l2_normalize(q) * HEAD_DIM ** -0.5
    k = l2_normalize(k)
    a = a.astype(jnp.float32).reshape(B, S, 2, DN_HEADS)
    b = b.astype(jnp.float32).reshape(B, S, 2, DN_HEADS)
    g = -jnp.exp(A_log.astype(jnp.float32)) * jax.nn.softplus(a + dt_bias.astype(jnp.float32))
    beta = jax.nn.sigmoid(b)
    return q, k, v, g, beta, z


def gated_delta_chunked(q, k, v, g, beta, state0):
    B, S, H, dk = q.shape
    dv = v.shape[-1]
    C = DN_CHUNK
    N = S // C

    def chunks(t):
        t = t.reshape((B, N, C, H) + t.shape[3:])
        return jnp.moveaxis(t, (1, 3), (0, 2))

    qc, kc, vc, gc, bc = [chunks(t) for t in (q, k, v, g, beta)]
    gcum = jnp.cumsum(gc, -1)
    idx = jnp.arange(C)
    incl = idx[:, None] >= idx[None, :]
    strict = idx[:, None] > idx[None, :]
    decay = jnp.exp(jnp.where(incl, gcum[..., :, None] - gcum[..., None, :], NEG_INF))
    kb = kc * bc[..., None]
    a_mat = jnp.where(strict, jnp.einsum('nbhid,nbhjd->nbhij', kb, kc) * decay, 0.0)
    rhs = jnp.concatenate([vc * bc[..., None], kb * jnp.exp(gcum)[..., None]], -1)
    sol = lax.linalg.triangular_solve(a_mat, rhs, left_side=True, lower=True, unit_diagonal=True)
    u, w = sol[..., :dv], sol[..., dv:]
    intra = jnp.where(incl, jnp.einsum('nbhid,nbhjd->nbhij', qc, kc) * decay, 0.0)
    q_dec = qc * jnp.exp(gcum)[..., None]
    k_dec = kc * jnp.exp(gcum[..., -1:] - gcum)[..., None]
    g_end = jnp.exp(gcum[..., -1])

    def step(state, xs):
        q_i, a_i, u_i, w_i, k_i, ge_i = xs
        v_new = u_i - jnp.einsum('bhcd,bhde->bhce', w_i, state)
        o_i = jnp.einsum('bhcd,bhde->bhce', q_i, state) + jnp.einsum('bhij,bhje->bhie', a_i, v_new)
        state = state * ge_i[..., None, None] + jnp.einsum('bhcd,bhce->bhde', k_i, v_new)
        return state, o_i

    state, o = lax.scan(step, state0, (q_dec, intra, u, w, k_dec, g_end))
    o = jnp.moveaxis(o, (0, 2), (1, 3)).reshape(B, S, H, dv)
    return o, state


def _flip(t, direction):
    return jnp.flip(t, 1) if direction == 1 else t


def gated_head_out(o, z, norm_g):
    B, S = o.shape[:2]
    zh = z.reshape(B, S, DN_HEADS, HEAD_DIM).astype(jnp.float32)
    return (rms_norm(o, norm_g) * jax.nn.silu(zh)).reshape(B, S, DN_DIM).astype(z.dtype)


def deltanet_mixer(p, pc, conv_w, A_log, dt_bias, norm_g, with_ctx):
    q, k, v, g, beta, z = deltanet_prep(p, conv_w, A_log, dt_bias)
    qc, kc, vc, gc, bc, zc = deltanet_prep(pc, conv_w, A_log, dt_bias)
    zero = jnp.zeros((p.shape[0], DN_HEADS, HEAD_DIM, HEAD_DIM), jnp.float32)
    o_lat, o_ctx = [], []
    for d in range(2):
        oc, state = gated_delta_chunked(_flip(qc, d), _flip(kc, d), _flip(vc, d),
                                        _flip(gc[:, :, d], d), _flip(bc[:, :, d], d), zero)
        ol, _ = gated_delta_chunked(_flip(q, d), _flip(k, d), _flip(v, d),
                                    _flip(g[:, :, d], d), _flip(beta[:, :, d], d), state)
        o_lat.append(_flip(ol, d))
        o_ctx.append(_flip(oc, d))
    y = gated_head_out(o_lat[0] + o_lat[1], z, norm_g)
    yc = gated_head_out(o_ctx[0] + o_ctx[1], zc, norm_g) if with_ctx else None
    return y, yc


def attn_heads(p, n_q, n_kv, q_gain, k_gain, rope):
    B, S, _ = p.shape
    q, k, v = _split(p, [n_q * HEAD_DIM, n_kv * HEAD_DIM, n_kv * HEAD_DIM])
    q = rms_norm(q.reshape(B, S, n_q, HEAD_DIM), q_gain)
    k = rms_norm(k.reshape(B, S, n_kv, HEAD_DIM), k_gain)
    v = v.reshape(B, S, n_kv, HEAD_DIM)
    if rope is not None:
        q = apply_rope(q, *rope)
        k = apply_rope(k, *rope)
    return q, k, v


def context_attention(qc, kc, vc, sink=None):
    B, L, Hq, d = qc.shape
    Hkv = kc.shape[2]
    G = Hq // Hkv
    qg = qc.reshape(B, L, Hkv, G, d)
    s = jnp.einsum('bqhgd,bkhd->bhgqk', qg, kc, preferred_element_type=jnp.float32) * d ** -0.5
    if sink is not None:
        sink_col = jnp.broadcast_to(sink.reshape(1, Hkv, G, 1, 1).astype(jnp.float32), s.shape[:-1] + (1,))
        s = jnp.concatenate([s, sink_col], -1)
    p = jax.nn.softmax(s, -1)[..., :L].astype(vc.dtype)
    return jnp.einsum('bhgqk,bkhd->bqhgd', p, vc).reshape(B, L, Hq * d)


def global_attention(q, k, v, kc, vc):
    B, S, Hq, d = q.shape
    Hkv = k.shape[2]
    G = Hq // Hkv
    nb = S // Q_BLOCK
    keys = jnp.concatenate([kc, k], 1)
    vals = jnp.concatenate([vc, v], 1)
    qb = jnp.moveaxis(q.reshape(B, nb, Q_BLOCK, Hkv, G, d), 1, 0)

    def one_block(qi):
        s = jnp.einsum('bqhgd,bkhd->bhgqk', qi, keys, preferred_element_type=jnp.float32) * d ** -0.5
        p = jax.nn.softmax(s, -1).astype(vals.dtype)
        return jnp.einsum('bhgqk,bkhd->bqhgd', p, vals)

    o = lax.map(one_block, qb)
    return jnp.moveaxis(o, 0, 1).reshape(B, S, Hq * d)


def window_attention(q, k, v, kc, vc, sink):
    B, S, Hq, d = q.shape
    Hkv = k.shape[2]
    G = Hq // Hkv
    L = kc.shape[1]
    nb = S // Q_BLOCK
    band = 3 * Q_BLOCK

    def bands(t):
        tp = jnp.pad(t, ((0, 0), (Q_BLOCK, Q_BLOCK), (0, 0), (0, 0))).reshape(B, nb + 2, Q_BLOCK, Hkv, d)
        return jnp.concatenate([tp[:, :-2], tp[:, 1:-1], tp[:, 2:]], axis=2)

    kb, vb = bands(k), bands(v)
    qb = q.reshape(B, nb, Q_BLOCK, Hkv, G, d)
    scale = d ** -0.5
    s_loc = jnp.einsum('bnqhgd,bnkhd->bnhgqk', qb, kb, preferred_element_type=jnp.float32) * scale
    s_ctx = jnp.einsum('bnqhgd,bkhd->bnhgqk', qb, kc, preferred_element_type=jnp.float32) * scale
    qpos = jnp.arange(nb)[:, None] * Q_BLOCK + jnp.arange(Q_BLOCK)[None]
    kpos = jnp.arange(nb)[:, None] * Q_BLOCK - Q_BLOCK + jnp.arange(band)[None]
    valid = ((jnp.abs(qpos[:, :, None] - kpos[:, None, :]) <= WINDOW)
             & (kpos[:, None, :] >= 0) & (kpos[:, None, :] < S))
    s_loc = jnp.where(valid[None, :, None, None], s_loc, NEG_INF)
    sink_col = jnp.broadcast_to(sink.reshape(1, 1, Hkv, G, 1, 1).astype(jnp.float32), s_loc.shape[:-1] + (1,))
    p = jax.nn.softmax(jnp.concatenate([s_loc, s_ctx, sink_col], -1), -1).astype(v.dtype)
    o = (jnp.einsum('bnhgqk,bnkhd->bnqhgd', p[..., :band], vb)
         + jnp.einsum('bnhgqk,bkhd->bnqhgd', p[..., band:band + L], vc))
    return o.reshape(B, S, Hq * d)


def fourier_mix(u):
    B, S, _ = u.shape
    uf = u.astype(jnp.float32).reshape(B, S, FT_GROUPS, FT_GROUP_DIM)
    y = jnp.fft.fft2(uf, axes=(1, 3), norm='ortho').real
    return y.reshape(B, S, FT_COLS).astype(u.dtype)


def hybrid_mixer(p, pc, rope, conv_w, A_log, dt_bias, dn_norm_g, ga_qn, ga_kn, wa_qn, wa_kn, sink, with_ctx):
    p_dn, p_ga, p_wa, p_ft = _split(p, [DN_COLS, GA_COLS, WA_COLS, FT_COLS])
    c_dn, c_ga, c_wa, c_ft = _split(pc, [DN_COLS, GA_COLS, WA_COLS, FT_COLS])
    y_dn, yc_dn = deltanet_mixer(p_dn, c_dn, conv_w, A_log, dt_bias, dn_norm_g, with_ctx)
    q1, k1, v1 = attn_heads(p_ga, GA_HEADS, GA_KV_HEADS, ga_qn, ga_kn, rope)
    qc1, kc1, vc1 = attn_heads(c_ga, GA_HEADS, GA_KV_HEADS, ga_qn, ga_kn, None)
    y_ga = global_attention(q1, k1, v1, kc1, vc1)
    q2, k2, v2 = attn_heads(p_wa, WA_HEADS, WA_KV_HEADS, wa_qn, wa_kn, rope)
    qc2, kc2, vc2 = attn_heads(c_wa, WA_HEADS, WA_KV_HEADS, wa_qn, wa_kn, None)
    y_wa = window_attention(q2, k2, v2, kc2, vc2, sink)
    y_ft = fourier_mix(p_ft)
    y = jnp.concatenate([y_dn, y_ga, y_wa, y_ft], -1)
    if not with_ctx:
        return y, None
    yc = jnp.concatenate([yc_dn, context_attention(qc1, kc1, vc1),
                          context_attention(qc2, kc2, vc2, sink), fourier_mix(c_ft)], -1)
    return y, yc


def setup_inputs(seed: int = 0) -> dict:
    key = jax.random.key(seed)
    ks = jax.random.split(key, 22)
    f32 = jnp.float32

    def normal(k, shape, scale):
        return jax.random.normal(k, shape, f32) * scale

    def gain(k, shape):
        return 1.0 + 0.02 * jax.random.normal(k, shape, f32)

    dt = jnp.exp(jax.random.uniform(ks[11], (DEPTH, 2, DN_HEADS), f32, math.log(1e-3), math.log(1e-1)))
    return {
        'x': normal(ks[0], (BATCH, SEQ, D_MODEL), 1.0),
        'c': normal(ks[1], (BATCH, D_MODEL), 1.0),
        'ctx': normal(ks[2], (BATCH, CTX_LEN, D_MODEL), 1.0),
        'c_ctx': normal(ks[3], (D_MODEL,), 1.0),
        'norm1_g': gain(ks[4], (DEPTH, D_MODEL)),
        'norm2_g': gain(ks[5], (DEPTH, D_MODEL)),
        'w_ada': normal(ks[6], (DEPTH, D_MODEL, N_MOD * D_MODEL), 0.5 * D_MODEL ** -0.5),
        'b_ada': normal(ks[7], (DEPTH, N_MOD * D_MODEL), 0.01),
        'w_in': normal(ks[8], (DEPTH, D_MODEL, IN_COLS), D_MODEL ** -0.5),
        'dn_conv_w': normal(ks[9], (DEPTH, CONV_W, 3 * DN_DIM), CONV_W ** -0.5),
        'dn_A_log': jnp.log(jax.random.uniform(ks[10], (DEPTH, 2, DN_HEADS), f32, 1.0, 16.0)),
        'dn_dt_bias': dt + jnp.log(-jnp.expm1(-dt)),
        'dn_norm_g': gain(ks[12], (DEPTH, HEAD_DIM)),
        'ga_q_norm': gain(ks[13], (DEPTH, HEAD_DIM)),
        'ga_k_norm': gain(ks[14], (DEPTH, HEAD_DIM)),
        'wa_q_norm': gain(ks[15], (DEPTH, HEAD_DIM)),
        'wa_k_norm': gain(ks[16], (DEPTH, HEAD_DIM)),
        'wa_sink': normal(ks[17], (DEPTH, WA_HEADS), 0.5),
        'w_out': normal(ks[18], (DEPTH, D_MIX, D_MODEL), D_MIX ** -0.5),
        'w_ffn_gate': normal(ks[19], (DEPTH, D_MODEL, FFN_HIDDEN), D_MODEL ** -0.5),
        'w_ffn_up': normal(ks[20], (DEPTH, D_MODEL, FFN_HIDDEN), D_MODEL ** -0.5),
        'w_ffn_down': normal(ks[21], (DEPTH, FFN_HIDDEN, D_MODEL), FFN_HIDDEN ** -0.5),
    }


def reference(x, c, ctx, c_ctx, norm1_g, norm2_g, w_ada, b_ada, w_in, dn_conv_w, dn_A_log, dn_dt_bias,
              dn_norm_g, ga_q_norm, ga_k_norm, wa_q_norm, wa_k_norm, wa_sink, w_out,
              w_ffn_gate, w_ffn_up, w_ffn_down):
    rope = axial_rope(x.shape[1])
    xc = ctx
    silu_c = jax.nn.silu(c)
    silu_cc = jax.nn.silu(c_ctx)
    for l in range(DEPTH):
        with_ctx = l < DEPTH - 1
        mod = (silu_c @ w_ada[l] + b_ada[l])[:, None, :]
        mod_c = silu_cc @ w_ada[l] + b_ada[l]
        sh1, sc1, gt1, sh2, sc2, gt2 = jnp.split(mod, N_MOD, -1)
        sh1c, sc1c, gt1c, sh2c, sc2c, gt2c = jnp.split(mod_c, N_MOD, -1)
        h = modulate(rms_norm(x, norm1_g[l]), sh1, sc1)
        hc = modulate(rms_norm(xc, norm1_g[l]), sh1c, sc1c)
        y, yc = hybrid_mixer(h @ w_in[l], hc @ w_in[l], rope, dn_conv_w[l], dn_A_log[l], dn_dt_bias[l],
                             dn_norm_g[l], ga_q_norm[l], ga_k_norm[l], wa_q_norm[l], wa_k_norm[l],
                             wa_sink[l], with_ctx)
        x = x + gt1 * (y @ w_out[l])
        x = x + gt2 * swiglu(modulate(rms_norm(x, norm2_g[l]), sh2, sc2),
                             w_ffn_gate[l], w_ffn_up[l], w_ffn_down[l])
        if with_ctx:
            xc = xc + gt1c * (yc @ w_out[l])
            xc = xc + gt2c * swiglu(modulate(rms_norm(xc, norm2_g[l]), sh2c, sc2c),
                                    w_ffn_gate[l], w_ffn_up[l], w_ffn_down[l])
    return x
```

```python
from contextlib import ExitStack
import numpy as np
import ml_dtypes
import concourse.bass as bass
import concourse.mybir as mybir
from concourse.bass_utils import run_bass_kernel_spmd

F32 = mybir.dt.float32
BF16 = mybir.dt.bfloat16
AF = mybir.ActivationFunctionType
ALU = mybir.AluOpType
AX = mybir.AxisListType

D = 1024
S = 4096
L = 256
T = S + L
DEPTH = 4
HD = 64
FFN = 2816
NJ = FFN // 128
EPS = 1e-6
NPCH = 19
NP = NPCH * 128

TT = [(0, L)] + [(L + i * 512, 512) for i in range(S // 512)]


class Buf:
    __slots__ = ("ap", "w", "r", "name", "psum")

    def __init__(self, ap, name="", psum=False):
        self.ap = ap
        self.w = None
        self.r = []
        self.name = name
        self.psum = psum

    def __getitem__(self, idx):
        return self.ap[idx]


class Sched:
    ENG = ("pe", "act", "dve", "pool", "sp")

    def __init__(self, nc):
        self.nc = nc
        self.e = {"pe": nc.tensor, "act": nc.scalar, "dve": nc.vector, "pool": nc.gpsimd, "sp": nc.sync}
        self.sems = {}
        self.cnt = {}
        for n in self.ENG:
            self.sems[n] = nc.alloc_semaphore("prog_" + n)
            self.cnt[n] = 0
        self.known = {n: {} for n in self.ENG}
        self.ninst = {n: 0 for n in self.ENG}
        self.pe_pending = False
        self.dq = {}
        for q, k in (("sp", 12), ("pool", 8), ("act", 4)):
            ring = []
            for i in range(k):
                key = "dma_%s_%d" % (q, i)
                self.sems[key] = nc.alloc_semaphore(key)
                self.cnt[key] = 0
                ring.append(key)
            self.dq[q] = [ring, 0]

    def _wait(self, eng, deps):
        best = {}
        for d in deps:
            if d is None:
                continue
            k, v = d
            if eng == "pe" and k == "pe":
                continue
            if best.get(k, 0) < v:
                best[k] = v
        kn = self.known[eng]
        for k, v in best.items():
            if kn.get(k, 0) < v:
                self.e[eng].wait_ge(self.sems[k], v)
                self.ninst[eng] += 1
                kn[k] = v

    def _deps(self, reads, writes):
        deps = []
        for t in reads:
            deps.append(t.w)
            if t.psum:
                deps.extend(t.r)
        for t in writes:
            deps.append(t.w)
            deps.extend(t.r)
        return deps

    def op(self, eng, fn, reads=(), writes=(), signal=True):
        self._wait(eng, self._deps(reads, writes))
        ins = fn(self.e[eng])
        self.ninst[eng] += 1
        if signal:
            self.cnt[eng] += 1
            ins.then_inc(self.sems[eng], 1)
            ev = (eng, self.cnt[eng])
            if eng == "pe":
                self.pe_pending = False
        else:
            assert eng == "pe"
            ev = (eng, self.cnt[eng] + 1)
            self.pe_pending = True
        for t in reads:
            t.r.append(ev)
            if len(t.r) > 64:
                t.r = self._compact(t.r)
        for t in writes:
            t.w = ev
            t.r = []
        return ins

    @staticmethod
    def _compact(evs):
        best = {}
        for k, v in evs:
            if best.get(k, 0) < v:
                best[k] = v
        return list(best.items())

    def dma(self, q, out_ap, in_ap, reads=(), writes=()):
        ring, idx = self.dq[q]
        key = ring[idx % len(ring)]
        self.dq[q][1] = idx + 1
        deps = self._deps(reads, writes)
        deps.append((key, self.cnt[key]))
        self._wait(q, deps)
        ins = self.e[q].dma_start(out=out_ap, in_=in_ap)
        self.ninst[q] += 1
        self.cnt[key] += 16
        ins.then_inc(self.sems[key], 16)
        ev = (key, self.cnt[key])
        for t in reads:
            t.r.append(ev)
        for t in writes:
            t.w = ev
            t.r = []
        return ins

    def barrier(self):
        assert not self.pe_pending
        for eng in self.ENG:
            kn = self.known[eng]
            for k, v in self.cnt.items():
                if k == eng or v == 0:
                    continue
                if kn.get(k, 0) < v:
                    self.e[eng].wait_ge(self.sems[k], v)
                    kn[k] = v

    def final_wait(self, eng="sp"):
        kn = self.known[eng]
        for k, v in self.cnt.items():
            if k == eng or v == 0:
                continue
            if kn.get(k, 0) < v:
                self.e[eng].wait_ge(self.sems[k], v)
                kn[k] = v


class Ctx:
    pass


class Mem:
    def __init__(self, g):
        self.nc = g.nc
        self.st = ExitStack()

    def __enter__(self):
        self.st.__enter__()
        return self

    def __exit__(self, *a):
        return self.st.__exit__(*a)

    def sb(self, name, shape, dt):
        return Buf(self.st.enter_context(self.nc.sbuf_tensor(U("sb_" + name), list(shape), dt)), name)

    def ps(self, name, shape, dt=None):
        dt = dt or F32
        per_bank = 512 if dt == F32 else 1024
        t = self.st.enter_context(self.nc.psum_tensor(U("pp_" + name), [128, per_bank], dt))
        shape = list(shape)
        n = 1
        for d in shape[1:]:
            n *= d
        assert n <= per_bank
        v = t[0:shape[0], 0:n]
        if len(shape) == 3:
            v = v.rearrange("p (a b) -> p a b", a=shape[1])
        elif len(shape) == 4:
            v = v.rearrange("p (a b c) -> p a b c", a=shape[1], b=shape[2])
        return Buf(v, name, psum=True)


_UID = [0]


def U(name):
    _UID[0] += 1
    return "%s_%d" % (name, _UID[0])


def build_nc(depth=DEPTH, debug=(), layer0=0):
    nc = bass.Bass("TRN2", target_bir_lowering=False)
    g = Ctx()
    g.nc = nc
    g.depth = depth
    g.layer0 = layer0
    g.debug = set(debug)

    def dram_in(name, shape, dt=F32):
        return nc.dram_tensor(name, list(shape), dt, kind="ExternalInput").ap()

    def dram_scratch(name, shape, dt=F32):
        kind = "ExternalOutput" if name in g.debug else "Internal"
        return Buf(nc.dram_tensor(name, list(shape), dt, kind=kind).ap(), name)

    g.dram_scratch = dram_scratch
    g.xT_in = dram_in("xT", [D, S])
    g.ctxT_in = dram_in("ctxT", [D, L])
    g.cT_in = dram_in("cT", [128, 8, 2])
    g.gains_in = dram_in("gains", [128, depth, 2, 8])
    g.w_ada_in = dram_in("w_ada", [depth, D, 6 * D])
    g.b_ada_in = dram_in("b_adaT", [128, depth, 48])
    g.w_in_in = dram_in("w_in", [depth, D, NP])
    g.rope_in = dram_in("ropeCS", [128, 2, S])
    g.cf32_in = dram_in("cf32", [128, 2, 128])
    g.hg_in = dram_in("hgains", [128, depth, 4])
    g.sink_in = dram_in("sinkbc", [128, depth * 4])
    g.wmask_in = dram_in("wmask", [128, 6, 512], BF16)
    g.dft_in = dram_in("dft", [8, 2, 128, 32, 512])
    g.dftc_in = dram_in("dftc", [2, 128, 2, 256])
    g.bdcs_in = dram_in("bdcs", [128, 2, 128])
    g.dmaskext_in = dram_in("dmaskext", [128, 2, 130])
    g.dsame_in = dram_in("dsame", [128, 128])
    g.dposmask_in = dram_in("dposmask", [128, 2, 2, 128])
    g.doffdiag_in = dram_in("doffdiag", [128, 4, 128])
    g.dsel4_in = dram_in("dsel4", [4, 2, 128])
    g.dsel16_in = dram_in("dsel16", [16, 2, 2, 128])
    g.drowmask_in = dram_in("drowmask", [16, 2])
    g.dnp_in = dram_in("dnp", [16, depth, 2])
    g.dconvw_in = dram_in("dconvw", [128, depth, 6, 3])
    g.dnormg_in = dram_in("dnormg", [128, depth])
    g.dqk = dram_scratch("dqk", [4, 128, T])
    g.dvk = dram_scratch("dvk", [T, 512])
    g.dgb = dram_scratch("dgb", [T, 16])
    g.dgbT = dram_scratch("dgbT", [16, T])
    g.do = dram_scratch("do", [2, T, 256])
    g.w_out_in = dram_in("w_out", [depth, D, D])
    g.w_gate_in = dram_in("w_gate", [depth, D, FFN])
    g.w_up_in = dram_in("w_up", [depth, D, FFN])
    g.w_down_in = dram_in("w_down", [depth, FFN, D])
    g.out = nc.dram_tensor("outT", [D, S], F32, kind="ExternalOutput").ap()
    g.ctxout = nc.dram_tensor("ctxoutT", [D, L], F32, kind="ExternalOutput").ap() if depth + layer0 < DEPTH else None
    g.qkT = dram_scratch("qkT", [2, 3, 128, T], BF16)
    g.vtok = dram_scratch("vtok", [2, T, 130], BF16)
    g.yT = dram_scratch("yT", [D, T], BF16)

    g.resid = dram_scratch("resid", [D, T])
    g.pT = dram_scratch("pT", [NP, T])
    g.modT_dbg = dram_scratch("modT", [128, 48, 2]) if "modT" in g.debug else None

    with nc.Block() as block:
        with nc.cleanup_on_exit():
            g.s = Sched(nc)
            emit_all(g)
    return nc


def emit_all(g):
    nc, s = g.nc, g.s
    depth = g.depth
    with (
        nc.sbuf_tensor(U("sb_ones_bf"), [128, 128], BF16) as ones_bf_t,
        nc.sbuf_tensor(U("sb_silu_cT"), [128, 8, 2], F32) as silu_cT_t,
        nc.sbuf_tensor(U("sb_gains"), [128, depth, 2, 8], F32) as gains_t,
        nc.sbuf_tensor(U("sb_b_ada"), [128, depth, 48], F32) as b_ada_t,
        nc.sbuf_tensor(U("sb_modT"), [128, 48, 2], F32) as modT_t,
        nc.sbuf_tensor(U("sb_modd"), [128, 4, 8, 2], F32) as modd_t,
        nc.sbuf_tensor(U("sb_bdones"), [128, 128], BF16) as bdones_t,
        nc.sbuf_tensor(U("sb_cf32"), [128, 2, 128], F32) as cf32_t,
        nc.sbuf_tensor(U("sb_onesf"), [128, 128], F32) as onesf_t,
        nc.sbuf_tensor(U("sb_hg"), [128, depth, 4], F32) as hg_t,
        nc.sbuf_tensor(U("sb_esink"), [128, depth * 4], F32) as esink_t,
    ):
        g.bdones = Buf(bdones_t, "bdones")
        g.cf32 = Buf(cf32_t, "cf32")
        g.onesf = Buf(onesf_t, "onesf")
        g.hg = Buf(hg_t, "hg")
        g.esink = Buf(esink_t, "esink")
        s.op("pool", lambda e: e.memset(g.bdones[:], 0.0), writes=[g.bdones])
        s.op("pool", lambda e: e.memset(g.bdones[0:64, 0:64], 1.0), writes=[g.bdones])
        s.op("pool", lambda e: e.memset(g.bdones[64:128, 64:128], 1.0), writes=[g.bdones])
        s.op("pool", lambda e: e.memset(g.onesf[:], 1.0), writes=[g.onesf])
        s.dma("sp", g.cf32[:], g.cf32_in[:, :, :], writes=[g.cf32])
        s.dma("sp", g.hg[:], g.hg_in[:, :, :], writes=[g.hg])
        s.dma("sp", g.esink[:], g.sink_in[:, :], writes=[g.esink])
        s.op("act", lambda e: e.activation(out=g.esink[:], in_=g.esink[:], func=AF.Exp), reads=[g.esink],
             writes=[g.esink])
        for col in (0, 2):
            s.op("dve", lambda e, col=col: e.tensor_scalar(out=g.hg[:, :, col], in0=g.hg[:, :, col], scalar1=0.125,
                                                          scalar2=None, op0=ALU.mult), reads=[g.hg], writes=[g.hg])

        g.ones_bf = Buf(ones_bf_t, "ones_bf")
        g.silu_cT = Buf(silu_cT_t, "silu_cT")
        g.gains = Buf(gains_t, "gains")
        g.b_ada = Buf(b_ada_t, "b_ada")
        g.modT = Buf(modT_t, "modT")
        g.modd = Buf(modd_t, "modd")

        s.op("pool", lambda e: e.memset(g.ones_bf[:], 1.0), writes=[g.ones_bf])
        s.dma("sp", g.silu_cT[:], g.cT_in[:, :, :], writes=[g.silu_cT])
        s.dma("sp", g.gains[:], g.gains_in[:, :, :, :], writes=[g.gains])
        s.dma("sp", g.b_ada[:], g.b_ada_in[:, :, :], writes=[g.b_ada])
        s.op("act", lambda e: e.activation(out=g.silu_cT[:], in_=g.silu_cT[:], func=AF.Silu),
             reads=[g.silu_cT], writes=[g.silu_cT])
        s.dma("sp", g.resid[:, 0:L], g.ctxT_in[:, :], writes=[g.resid])
        s.dma("sp", g.resid[:, L:T], g.xT_in[:, :], writes=[g.resid])

        base_debug = set(g.debug)
        for l in range(depth):
            g.debug = set(base_debug) | ({f[3:] for f in base_debug if f.startswith("L1:")} if l >= 1 else set())
            stage_mod(g, l)
            s.barrier()
            stage_inproj(g, l)
            s.barrier()
            if "noattn" not in g.debug:
                stage_qkprep(g, l)
                s.barrier()
                if "nodn" not in g.debug:
                    stage_dn_prep(g, l)
                    s.barrier()
                    if "dn_prep_only" not in g.debug:
                        stage_dn_chain(g, l)
                        s.barrier()
                        if "dn_noout" not in g.debug:
                            stage_dn_out(g, l)
                            s.barrier()
                if "noft" not in g.debug:
                    stage_fourier(g, l)
                    s.barrier()
                for mix in range(2):
                    if "noattn2" in g.debug or ("attn_mix%d" % (1 - mix)) in g.debug:
                        continue
                    stage_attn(g, l, mix)
                    s.barrier()
            if "notail" not in g.debug:
                stage_outproj(g, l)
                s.barrier()
                stage_ffn(g, l)
                s.barrier()

        s.dma("sp", g.out[:, :], g.resid[:, L:T], reads=[g.resid])
        if g.ctxout is not None:
            s.dma("sp", g.ctxout[:, :], g.resid[:, 0:L], reads=[g.resid])
        s.final_wait("sp")
        s.final_wait("pool")


def stage_mod(g, l):
    nc, s = g.nc, g.s
    with (
        nc.sbuf_tensor(U("sb_wada0"), [128, 8, 1024], F32) as wa0,
        nc.sbuf_tensor(U("sb_wada1"), [128, 8, 1024], F32) as wa1,
        nc.psum_tensor(U("pp_modps"), [128, 48, 2], F32) as mps_t,
    ):
        wbuf = [Buf(wa0, "wada0"), Buf(wa1, "wada1")]
        mps = Buf(mps_t, "modps", psum=True)
        for slot in range(6):
            wb = wbuf[slot % 2]
            src = g.w_ada_in[l, :, slot * 1024:(slot + 1) * 1024].rearrange("(k p) n -> p k n", p=128)
            s.dma("sp", wb[:, 0:4, :], src[:, 0:4, :], writes=[wb])
            s.dma("sp", wb[:, 4:8, :], src[:, 4:8, :], writes=[wb])
            for m in range(8):
                for k in range(8):
                    s.op("pe", lambda e, m=m, k=k, wb=wb, slot=slot: e.matmul(
                        mps[:, slot * 8 + m, :], wb[:, k, m * 128:(m + 1) * 128], g.silu_cT[:, k, :],
                        start=(k == 0), stop=(k == 7)),
                        reads=[wb, g.silu_cT], writes=[mps], signal=(k == 7))
        for w in range(2):
            s.op("dve", lambda e, w=w: e.tensor_tensor(out=g.modT[:, :, w], in0=mps[:, :, w], in1=g.b_ada[:, l, :],
                                                       op=ALU.add),
                 reads=[mps, g.b_ada], writes=[g.modT])
        if g.modT_dbg is not None:
            s.dma("sp", g.modT_dbg[:, :, :], g.modT[:], reads=[g.modT], writes=[g.modT_dbg])
        for w in range(2):
            for (dst, gi, scs, shs) in ((0, 0, 1, 0), (2, 1, 4, 3)):
                s.op("dve", lambda e, w=w, dst=dst, gi=gi, scs=scs: e.scalar_tensor_tensor(
                    out=g.modd[:, dst, :, w], in0=g.modT[:, scs * 8:(scs + 1) * 8, w], scalar=1.0,
                    in1=g.gains[:, l, gi, :], op0=ALU.add, op1=ALU.mult),
                    reads=[g.modT, g.gains], writes=[g.modd])
                s.op("dve", lambda e, w=w, dst=dst, shs=shs: e.tensor_copy(
                    out=g.modd[:, dst + 1, :, w], in_=g.modT[:, shs * 8:(shs + 1) * 8, w]),
                    reads=[g.modT], writes=[g.modd])


def norm_mod_tile(g, xt, hT, which, dsel, n, ps_bank, sq, rstd, tmps):
    s = g.s
    s.op("act", lambda e: e.activation(out=sq[:, 0:8, :n], in_=xt[:, :, :n], func=AF.Square), reads=[xt], writes=[sq])
    for k in range(8):
        s.op("pe", lambda e, k=k: e.matmul(ps_bank[:, :n], g.ones_bf[:, :], sq[:, k, :n], start=(k == 0), stop=(k == 7)),
             reads=[g.ones_bf, sq], writes=[ps_bank], signal=(k == 7))
    s.op("dve", lambda e: e.tensor_scalar(out=rstd[:, :n], in0=ps_bank[:, :n], scalar1=1.0 / D, scalar2=EPS,
                                          op0=ALU.mult, op1=ALU.add), reads=[ps_bank], writes=[rstd])
    s.op("act", lambda e: e.activation(out=rstd[:, :n], in_=rstd[:, :n], func=AF.Sqrt), reads=[rstd], writes=[rstd])
    s.op("dve", lambda e: e.reciprocal(out=rstd[:, :n], in_=rstd[:, :n]), reads=[rstd], writes=[rstd])
    for k in range(8):
        tmp = tmps[k % len(tmps)]
        eng = "dve" if k % 2 == 0 else "pool"
        s.op(eng, lambda e, k=k, tmp=tmp: e.tensor_tensor(out=tmp[:, :n], in0=xt[:, k, :n], in1=rstd[:, :n], op=ALU.mult),
             reads=[xt, rstd], writes=[tmp])
        s.op("act", lambda e, k=k, tmp=tmp: e.activation(out=hT[:, k, :n], in_=tmp[:, :n], func=AF.Identity,
                                                         scale=g.modd[:, dsel, k, which:which + 1],
                                                         bias=g.modd[:, dsel + 1, k, which:which + 1]),
             reads=[tmp, g.modd], writes=[hT])


def stage_inproj(g, l):
    nc, s = g.nc, g.s
    with (
        nc.sbuf_tensor(U("sb_w_in"), [128, 8, NP], BF16) as w_in_t,
        nc.sbuf_tensor(U("sb_xt0"), [128, 8, 512], F32) as xt0, nc.sbuf_tensor(U("sb_xt1"), [128, 8, 512], F32) as xt1,
        nc.sbuf_tensor(U("sb_hT0"), [128, 8, 512], BF16) as hT0, nc.sbuf_tensor(U("sb_hT1"), [128, 8, 512], BF16) as hT1,
        nc.sbuf_tensor(U("sb_sq"), [128, 8, 512], BF16) as sq_t,
        nc.sbuf_tensor(U("sb_rstd"), [128, 512], F32) as rstd_t,
        nc.sbuf_tensor(U("sb_tmpa"), [128, 512], F32) as tmpa_t, nc.sbuf_tensor(U("sb_tmpb"), [128, 512], F32) as tmpb_t,
        nc.sbuf_tensor(U("sb_po0"), [128, 512], F32) as po0, nc.sbuf_tensor(U("sb_po1"), [128, 512], F32) as po1,
        nc.sbuf_tensor(U("sb_po2"), [128, 512], F32) as po2, nc.sbuf_tensor(U("sb_po3"), [128, 512], F32) as po3,
        nc.psum_tensor(U("pp_ps0"), [128, 512], F32) as ps0, nc.psum_tensor(U("pp_ps1"), [128, 512], F32) as ps1,
        nc.psum_tensor(U("pp_ps2"), [128, 512], F32) as ps2, nc.psum_tensor(U("pp_ps3"), [128, 512], F32) as ps3,
        nc.psum_tensor(U("pp_psn"), [128, 512], F32) as psn,
    ):
        w_in = Buf(w_in_t, "w_in")
        xts = [Buf(xt0), Buf(xt1)]
        hTs = [Buf(hT0), Buf(hT1)]
        sq = Buf(sq_t)
        rstd = Buf(rstd_t)
        tmps = [Buf(tmpa_t), Buf(tmpb_t)]
        pos = [Buf(po0), Buf(po1), Buf(po2), Buf(po3)]
        pss = [Buf(ps0, psum=True), Buf(ps1, psum=True), Buf(ps2, psum=True), Buf(ps3, psum=True)]
        psn_b = Buf(psn, psum=True)
        src = g.w_in_in[l].rearrange("(k p) n -> p k n", p=128)
        for k in range(8):
            s.dma("pool", w_in[:, k, :], src[:, k, :], writes=[w_in])
        cnt = 0
        for ti, (t0, n) in enumerate(TT):
            xt = xts[ti % 2]
            hT = hTs[ti % 2]
            which = 1 if ti == 0 else 0
            s.dma("sp", xt[:, :, :n], g.resid[:, t0:t0 + n].rearrange("(k p) n -> p k n", p=128), reads=[g.resid],
                  writes=[xt])
            norm_mod_tile(g, xt, hT, which, 0, n, psn_b, sq, rstd, tmps)
            for m in range(NPCH):
                rows = 16 if m == 8 else 128
                ps = pss[cnt % 4]
                po = pos[cnt % 4]
                cnt += 1
                for k in range(8):
                    s.op("pe", lambda e, k=k, m=m, ps=ps, rows=rows: e.matmul(
                        ps[:rows, :n], w_in[:, k, m * 128:m * 128 + rows], hT[:, k, :n], start=(k == 0), stop=(k == 7)),
                        reads=[w_in, hT], writes=[ps], signal=(k == 7))
                ev_eng = "act" if cnt % 2 == 0 else "dve"
                if ev_eng == "act":
                    s.op("act", lambda e, ps=ps, po=po, rows=rows: e.copy(out=po[:rows, :n], in_=ps[:rows, :n]),
                         reads=[ps], writes=[po])
                else:
                    s.op("dve", lambda e, ps=ps, po=po, rows=rows: e.tensor_copy(out=po[:rows, :n], in_=ps[:rows, :n]),
                         reads=[ps], writes=[po])
                s.dma("sp", g.pT[m * 128:m * 128 + rows, t0:t0 + n], po[:rows, :n], reads=[po], writes=[g.pT])


def stage_qkprep(g, l):
    nc, s = g.nc, g.s
    with Mem(g) as mem:
        rope = mem.sb("rope", [128, 2, S], F32)
        pins = [mem.sb("pin%d" % i, [128, 4, 512], F32) for i in range(2)]
        sqs = [mem.sb("sq%d" % i, [128, 512], BF16) for i in range(2)]
        rss = [mem.sb("rs%d" % i, [128, 512], F32) for i in range(2)]
        pns = [mem.sb("pn%d" % i, [128, 512], F32) for i in range(2)]
        t1s = [mem.sb("t1%d" % i, [128, 512], F32) for i in range(2)]
        t2s = [mem.sb("t2%d" % i, [128, 512], F32) for i in range(2)]
        obs = [mem.sb("ob%d" % i, [128, 512], BF16) for i in range(3)]
        vts = [mem.sb("vt%d" % i, [128, 4, 2, 65], BF16) for i in range(2)]
        sss = [mem.ps("ss%d" % i, [128, 512]) for i in range(2)]
        rts = [mem.ps("rt%d" % i, [128, 512]) for i in range(2)]
        vps = [mem.ps("vp%d" % i, [128, 4, 128]) for i in range(2)]
        s.dma("sp", rope[:, 0, :], g.rope_in[:, 0, :], writes=[rope])
        s.dma("sp", rope[:, 1, :], g.rope_in[:, 1, :], writes=[rope])
        for vt in vts:
            s.op("pool", lambda e, vt=vt: e.memset(vt[:], 1.0), writes=[vt])
        it = 0
        ic = 0
        for mix in range(2):
            m0 = 9 + 4 * mix
            for ti, (t0, n) in enumerate(TT):
                pin = pins[it % 2]
                vt = vts[it % 2]
                vp = vps[it % 2]
                it += 1
                s.dma("sp", pin[:, :, :n], g.pT[m0 * 128:(m0 + 4) * 128, t0:t0 + n].rearrange("(c p) n -> p c n", p=128),
                      reads=[g.pT], writes=[pin])
                for c in range(3):
                    sq, rs, pn, t1, t2, ss, rt = (sqs[ic % 2], rss[ic % 2], pns[ic % 2], t1s[ic % 2], t2s[ic % 2],
                                                  sss[ic % 2], rts[ic % 2])
                    ob = obs[ic % 3]
                    ic += 1
                    gcol = g.hg[:, l, 2 * mix + (0 if c < 2 else 1):2 * mix + (0 if c < 2 else 1) + 1]
                    s.op("act", lambda e, c=c, sq=sq: e.activation(out=sq[:, :n], in_=pin[:, c, :n], func=AF.Square),
                         reads=[pin], writes=[sq])
                    s.op("pe", lambda e, sq=sq, ss=ss: e.matmul(ss[:, :n], g.bdones[:, :], sq[:, :n], start=True, stop=True),
                         reads=[g.bdones, sq], writes=[ss])
                    s.op("dve", lambda e, rs=rs, ss=ss: e.tensor_scalar(out=rs[:, :n], in0=ss[:, :n], scalar1=1.0 / HD,
                                                                        scalar2=EPS, op0=ALU.mult, op1=ALU.add),
                         reads=[ss], writes=[rs])
                    s.op("act", lambda e, rs=rs: e.activation(out=rs[:, :n], in_=rs[:, :n], func=AF.Sqrt), reads=[rs],
                         writes=[rs])
                    s.op("dve", lambda e, rs=rs: e.reciprocal(out=rs[:, :n], in_=rs[:, :n]), reads=[rs], writes=[rs])
                    s.op("dve", lambda e, c=c, pn=pn, rs=rs, gcol=gcol: e.scalar_tensor_tensor(
                        out=pn[:, :n], in0=pin[:, c, :n], scalar=gcol, in1=rs[:, :n], op0=ALU.mult, op1=ALU.mult),
                        reads=[pin, rs, g.hg], writes=[pn])
                    if ti > 0:
                        p0 = t0 - L
                        s.op("pe", lambda e, pn=pn, rt=rt: e.matmul(rt[:, :n], g.cf32[:, 1, :], pn[:, :n], start=True, stop=True),
                             reads=[g.cf32, pn], writes=[rt])
                        s.op("pool", lambda e, pn=pn, t1=t1, p0=p0: e.tensor_tensor(out=t1[:, :n], in0=pn[:, :n],
                                                                                   in1=rope[:, 0, p0:p0 + n], op=ALU.mult),
                             reads=[pn, rope], writes=[t1])
                        s.op("dve", lambda e, rt=rt, t2=t2, p0=p0: e.tensor_tensor(out=t2[:, :n], in0=rt[:, :n],
                                                                                  in1=rope[:, 1, p0:p0 + n], op=ALU.mult),
                             reads=[rt, rope], writes=[t2])
                        s.op("pool", lambda e, t1=t1, t2=t2, ob=ob: e.tensor_tensor(out=ob[:, :n], in0=t1[:, :n], in1=t2[:, :n],
                                                                                   op=ALU.add),
                             reads=[t1, t2], writes=[ob])
                    else:
                        s.op("pool", lambda e, pn=pn, ob=ob: e.tensor_copy(out=ob[:, :n], in_=pn[:, :n]), reads=[pn], writes=[ob])
                    s.dma("sp", g.qkT[mix, c, :, t0:t0 + n], ob[:, :n], reads=[ob], writes=[g.qkT])
                nj = n // 128
                for j in range(nj):
                    s.op("pe", lambda e, j=j, vp=vp: e.transpose(vp[:, j, :], pin[:, 3, j * 128:(j + 1) * 128], g.cf32[:, 0, :]),
                         reads=[pin, g.cf32], writes=[vp], signal=(j == nj - 1))
                s.op("act", lambda e, vt=vt, vp=vp, nj=nj: e.copy(
                    out=vt[:, :nj, :, 0:64], in_=vp[:, :nj, :].rearrange("p j (k d) -> p j k d", k=2)),
                    reads=[vp], writes=[vt])
                s.dma("sp", g.vtok[mix, t0:t0 + n, :].rearrange("(j p) c -> p j c", p=128),
                      vt[:, :nj, :, :].rearrange("p j k d -> p j (k d)"), reads=[vt], writes=[g.vtok])


def stage_attn(g, l, mix):
    nc, s = g.nc, g.s
    with_ctx = (l + g.layer0) < DEPTH - 1
    NKT = T // 128
    with Mem(g) as mem:
        kT = mem.sb("kT", [128, T], BF16)
        vtk = mem.sb("vtk", [128, NKT, 130], BF16)
        wm = mem.sb("wm", [128, 6, 512], BF16)
        qts = [mem.sb("qt%d" % i, [128, 2, 512], BF16) for i in range(2)]
        pts = [mem.sb("pt%d" % i, [128, 512], BF16) for i in range(3)]
        rdens = [mem.sb("rden%d" % i, [128, 512], F32) for i in range(2)]
        osbs = [mem.sb("osb%d" % i, [64, 512], F32) for i in range(2)]
        yos = [mem.sb("yo%d" % i, [64, 512], BF16) for i in range(2)]
        sps = [mem.ps("s%d" % i, [128, 512]) for i in range(3)]
        ops = [mem.ps("o%d" % i, [128, 512]) for i in range(2)]
        bcs = [mem.ps("bc%d" % i, [64, 512]) for i in range(2)]
        s.dma("sp", kT[:, :], g.qkT[mix, 2, :, :], reads=[g.qkT], writes=[kT])
        half = NKT // 2
        s.dma("sp", vtk[:, 0:half, :], g.vtok[mix, 0:half * 128, :].rearrange("(j p) c -> p j c", p=128),
              reads=[g.vtok], writes=[vtk])
        s.dma("sp", vtk[:, half:NKT, :], g.vtok[mix, half * 128:T, :].rearrange("(j p) c -> p j c", p=128),
              reads=[g.vtok], writes=[vtk])
        if mix == 1:
            s.dma("sp", wm[:], g.wmask_in[:, :, :], writes=[wm])
        si = 0
        hi = 0
        for ti, (t0, n) in enumerate(TT):
            if ti == 0 and not with_ctx:
                continue
            if "attn_loads_only" in g.debug:
                continue
            if "attn_t1" in g.debug and ti > 1:
                continue
            if "attn_t2" in g.debug and ti > 2:
                continue
            if "attn_t4" in g.debug and ti > 4:
                continue
            qt = qts[ti % 2]
            s.dma("sp", qt[:, :, :n], g.qkT[mix, 0:2, :, t0:t0 + n].rearrange("c p n -> p c n"), reads=[g.qkT], writes=[qt])
            if ti == 0:
                keys = [(0, None), (1, None)]
            elif mix == 0:
                keys = [(kt, None) for kt in range(NKT)]
            else:
                tq = ti - 1
                keys = [(0, None), (1, None)]
                for r in range(6):
                    j = 4 * tq - 1 + r
                    if 0 <= j < S // 128:
                        keys.append((2 + j, r))
            nk = len(keys)
            for h in range(4):
                c = h % 2
                kv = h // 2
                pb = 64 * kv
                opb = ops[hi % 2]
                rden = rdens[hi % 2]
                osb = osbs[hi % 2]
                yo = yos[hi % 2]
                bc = bcs[hi % 2]
                hi += 1

                def qk(i, si0):
                    kt = keys[i][0]
                    sp = sps[(si0 + i) % 3]
                    s.op("pe", lambda e: e.matmul(sp[:, :n], kT[pb:pb + 64, kt * 128:(kt + 1) * 128], qt[pb:pb + 64, c, :n],
                                                  start=True, stop=True), reads=[kT, qt], writes=[sp])

                qk(0, si)
                for i in range(nk):
                    if i + 1 < nk:
                        qk(i + 1, si)
                    kt, mr = keys[i]
                    sp = sps[(si + i) % 3]
                    pt = pts[(si + i) % 3]
                    s.op("act", lambda e, sp=sp, pt=pt: e.activation(out=pt[:, :n], in_=sp[:, :n], func=AF.Exp),
                         reads=[sp], writes=[pt])
                    if mr is not None:
                        s.op("pool", lambda e, pt=pt, mr=mr: e.tensor_tensor(out=pt[:, :n], in0=pt[:, :n], in1=wm[:, mr, :n],
                                                                            op=ALU.mult), reads=[pt, wm], writes=[pt])
                    s.op("pe", lambda e, pt=pt, kt=kt, i=i: e.matmul(opb[0:65, :n], vtk[:, kt, kv * 65:(kv + 1) * 65], pt[:, :n],
                                                                     start=(i == 0), stop=(i == nk - 1)),
                         reads=[vtk, pt], writes=[opb], signal=(i == nk - 1))
                si += nk
                if mix == 1:
                    s.op("dve", lambda e: e.tensor_scalar(out=rden[64:65, :n], in0=opb[64:65, :n],
                                                          scalar1=g.esink[64:65, l * 4 + h:l * 4 + h + 1], scalar2=None,
                                                          op0=ALU.add), reads=[opb, g.esink], writes=[rden])
                    s.op("dve", lambda e: e.reciprocal(out=rden[64:65, :n], in_=rden[64:65, :n]), reads=[rden], writes=[rden])
                else:
                    s.op("dve", lambda e: e.reciprocal(out=rden[64:65, :n], in_=opb[64:65, :n]), reads=[opb], writes=[rden])
                s.op("pe", lambda e: e.matmul(bc[0:64, :n], g.onesf[64:65, 0:64], rden[64:65, :n], start=True, stop=True),
                     reads=[g.onesf, rden], writes=[bc])
                s.op("act", lambda e: e.copy(out=osb[:, :n], in_=opb[0:64, :n]), reads=[opb], writes=[osb])
                s.op("dve", lambda e: e.tensor_tensor(out=yo[:, :n], in0=osb[:, :n], in1=bc[0:64, :n], op=ALU.mult),
                     reads=[osb, bc], writes=[yo])
                row0 = 256 * (1 + mix) + 64 * h
                s.dma("sp", g.yT[row0:row0 + 64, t0:t0 + n], yo[:, :n], reads=[yo], writes=[g.yT])


def stage_fourier(g, l):
    nc, s = g.nc, g.s
    with_ctx = (l + g.layer0) < DEPTH - 1
    NTK = T // 128
    NQ = 8
    with Mem(g) as mem:
        uT = mem.sb("uT", [128, 2, T], F32)
        bdcs = mem.sb("bdcs", [128, 2, 128], F32)
        ucus = mem.sb("ucus", [128, NTK, 2, 2, 128], F32)
        dbuf = [[mem.sb("dft%d%d" % (w, i), [128, NQ, 512], F32) for i in range(2)] for w in range(2)]
        dctx = mem.sb("dctx", [128, 2, 2, 256], F32)
        yo = [mem.sb("yft%d" % i, [128, 512], BF16) for i in range(2)]
        pu = [mem.ps("pu%d" % i, [128, 2, 128]) for i in range(2)]
        py = [mem.ps("py%d" % i, [128, 512]) for i in range(4)]
        for cc in range(2):
            s.dma("sp", uT[:, cc, :], g.pT[(17 + cc) * 128:(18 + cc) * 128, :], reads=[g.pT], writes=[uT])
        s.dma("sp", bdcs[:], g.bdcs_in[:, :, :], writes=[bdcs])
        s.dma("sp", dctx[:], g.dftc_in.rearrange("w p j n -> p w j n"), writes=[dctx])
        k = 0
        for j in range(NTK):
            for cc in range(2):
                p = pu[k % 2]
                k += 1
                s.op("pe", lambda e, p=p, j=j, cc=cc: e.matmul(p[:, :, :].rearrange("p a b -> p (a b)"),
                                                              uT[:, cc, j * 128:(j + 1) * 128],
                                                              bdcs[:, :, :].rearrange("p a b -> p (a b)"), start=True, stop=True),
                     reads=[uT, bdcs], writes=[p])
                eng = "act" if k % 2 == 0 else "dve"
                if eng == "act":
                    s.op("act", lambda e, p=p, j=j, cc=cc: e.copy(out=ucus[:, j, :, cc, :], in_=p[:, :, :]), reads=[p], writes=[ucus])
                else:
                    s.op("dve", lambda e, p=p, j=j, cc=cc: e.tensor_copy(out=ucus[:, j, :, cc, :], in_=p[:, :, :]), reads=[p], writes=[ucus])
        oi = 0
        if with_ctx:
            for cc in range(2):
                ps = py[oi % 4]
                y = yo[oi % 2]
                oi += 1
                for sch in range(2):
                    for w in range(2):
                        s.op("pe", lambda e, ps=ps, sch=sch, w=w, cc=cc: e.matmul(
                            ps[:, :L], ucus[:, sch, w, cc, :], dctx[:, w, sch, :], start=(sch == 0 and w == 0),
                            stop=(sch == 1 and w == 1)), reads=[ucus, dctx], writes=[ps], signal=(sch == 1 and w == 1))
                s.op("act", lambda e, ps=ps, y=y: e.copy(out=y[:, :L], in_=ps[:, :L]), reads=[ps], writes=[y])
                s.dma("sp", g.yT[768 + cc * 128:896 + cc * 128, 0:L], y[:, :L], reads=[y], writes=[g.yT])
        di = 0
        nblk = 32 // NQ
        for st in range(S // 512):
            pss = [py[(oi + cc) % 4] for cc in range(2)]
            for hf in range(nblk):
                bufs = [dbuf[w][di % 2] for w in range(2)]
                di += 1
                for w in range(2):
                    s.dma("sp" if w == 0 else "act", bufs[w][:], g.dft_in[st, w, :, hf * NQ:(hf + 1) * NQ, :], writes=[bufs[w]])
                for j in range(NQ):
                    sch = hf * NQ + j
                    for cc in range(2):
                        for w in range(2):
                            first = (sch == 0 and w == 0)
                            last = (sch == 31 and w == 1)
                            s.op("pe", lambda e, cc=cc, w=w, j=j, sch=sch, first=first, last=last: e.matmul(
                                pss[cc][:, :], ucus[:, 2 + sch, w, cc, :], bufs[w][:, j, :], start=first, stop=last),
                                reads=[ucus, bufs[w]], writes=[pss[cc]], signal=(last or (j == NQ - 1 and w == 1 and cc == 1)))
            for cc in range(2):
                y = yo[oi % 2]
                oi += 1
                if cc == 0:
                    s.op("act", lambda e, y=y: e.copy(out=y[:, :], in_=pss[0][:, :]), reads=[pss[0]], writes=[y])
                else:
                    s.op("dve", lambda e, y=y: e.tensor_copy(out=y[:, :], in_=pss[1][:, :]), reads=[pss[1]], writes=[y])
                s.dma("sp", g.yT[768 + cc * 128:896 + cc * 128, L + st * 512:L + (st + 1) * 512], y[:, :], reads=[y],
                      writes=[g.yT])


def stage_outproj(g, l):
    nc, s = g.nc, g.s
    with_ctx = (l + g.layer0) < DEPTH - 1
    with Mem(g) as mem:
        w = mem.sb("w_out", [128, 8, D], BF16)
        xts = [mem.sb("xo%d" % i, [128, 8, 512], F32) for i in range(2)]
        yts = [mem.sb("yo%d" % i, [128, 8, 512], BF16) for i in range(2)]
        pss = [mem.ps("po%d" % i, [128, 512]) for i in range(4)]
        src = g.w_out_in[l].rearrange("(k p) n -> p k n", p=128)
        for k in range(8):
            s.dma("pool", w[:, k, :], src[:, k, :], writes=[w])
        cnt = 0
        for ti, (t0, n) in enumerate(TT):
            if ti == 0 and not with_ctx:
                continue
            which = 1 if ti == 0 else 0
            xt = xts[ti % 2]
            yt = yts[ti % 2]
            s.dma("sp", xt[:, :, :n], g.resid[:, t0:t0 + n].rearrange("(k p) n -> p k n", p=128), reads=[g.resid], writes=[xt])
            s.dma("sp", yt[:, :, :n], g.yT[:, t0:t0 + n].rearrange("(k p) n -> p k n", p=128), reads=[g.yT], writes=[yt])
            for m in range(8):
                ps = pss[cnt % 4]
                cnt += 1
                for k in range(8):
                    s.op("pe", lambda e, k=k, m=m, ps=ps: e.matmul(ps[:, :n], w[:, k, m * 128:(m + 1) * 128], yt[:, k, :n],
                                                                  start=(k == 0), stop=(k == 7)),
                         reads=[w, yt], writes=[ps], signal=(k == 7))
                s.op("dve", lambda e, m=m, ps=ps: e.scalar_tensor_tensor(
                    out=xt[:, m, :n], in0=ps[:, :n], scalar=g.modT[:, 2 * 8 + m, which:which + 1], in1=xt[:, m, :n],
                    op0=ALU.mult, op1=ALU.add), reads=[ps, xt, g.modT], writes=[xt])
            s.dma("sp", g.resid[:, t0:t0 + n].rearrange("(k p) n -> p k n", p=128), xt[:, :, :n], reads=[xt], writes=[g.resid])


def stage_ffn(g, l):
    nc, s = g.nc, g.s
    with_ctx = (l + g.layer0) < DEPTH - 1
    with Mem(g) as mem:
        wg = mem.sb("w_gate", [128, 8, FFN], BF16)
        wu = mem.sb("w_up", [128, 8, FFN], BF16)
        wd = mem.sb("w_down", [128, NJ, D], BF16)
        xt = mem.sb("xf", [128, 8, 512], F32)
        hT = mem.sb("hf", [128, 8, 512], BF16)
        act = mem.sb("actf", [128, NJ, 512], BF16)
        rstd = mem.sb("rstdf", [128, 512], F32)
        tmps = [mem.sb("tmpf%d" % i, [128, 512], F32) for i in range(2)]
        sgs = [mem.sb("sg%d" % i, [128, 512], F32) for i in range(2)]
        psn = mem.ps("pfn", [128, 512])
        psg = [mem.ps("pfg%d" % i, [128, 512]) for i in range(2)]
        psu = [mem.ps("pfu%d" % i, [128, 512]) for i in range(2)]
        psd = [mem.ps("pfd%d" % i, [128, 512]) for i in range(2)]
        for (wt, src_in) in ((wg, g.w_gate_in), (wu, g.w_up_in)):
            src = src_in[l].rearrange("(k p) n -> p k n", p=128)
            for k in range(8):
                s.dma("pool", wt[:, k, :], src[:, k, :], writes=[wt])
        src = g.w_down_in[l].rearrange("(j p) n -> p j n", p=128)
        for j in range(0, NJ, 2):
            s.dma("pool", wd[:, j:j + 2, :], src[:, j:j + 2, :], writes=[wd])
        for ti, (t0, n) in enumerate(TT):
            if ti == 0 and not with_ctx:
                continue
            which = 1 if ti == 0 else 0
            s.dma("sp", xt[:, :, :n], g.resid[:, t0:t0 + n].rearrange("(k p) n -> p k n", p=128), reads=[g.resid], writes=[xt])
            norm_mod_tile(g, xt, hT, which, 2, n, psn, act, rstd, tmps)
            for j in range(NJ):
                pg, pu, sg = psg[j % 2], psu[j % 2], sgs[j % 2]
                for k in range(8):
                    s.op("pe", lambda e, k=k, j=j, pg=pg: e.matmul(pg[:, :n], wg[:, k, j * 128:(j + 1) * 128], hT[:, k, :n],
                                                                  start=(k == 0), stop=(k == 7)),
                         reads=[wg, hT], writes=[pg], signal=(k == 7))
                for k in range(8):
                    s.op("pe", lambda e, k=k, j=j, pu=pu: e.matmul(pu[:, :n], wu[:, k, j * 128:(j + 1) * 128], hT[:, k, :n],
                                                                  start=(k == 0), stop=(k == 7)),
                         reads=[wu, hT], writes=[pu], signal=(k == 7))
                s.op("act", lambda e, pg=pg, sg=sg: e.activation(out=sg[:, :n], in_=pg[:, :n], func=AF.Silu), reads=[pg], writes=[sg])
                s.op("dve", lambda e, j=j, pu=pu, sg=sg: e.tensor_tensor(out=act[:, j, :n], in0=sg[:, :n], in1=pu[:, :n], op=ALU.mult),
                     reads=[sg, pu], writes=[act])
            for m in range(8):
                pd = psd[m % 2]
                for j in range(NJ):
                    s.op("pe", lambda e, j=j, m=m, pd=pd: e.matmul(pd[:, :n], wd[:, j, m * 128:(m + 1) * 128], act[:, j, :n],
                                                                  start=(j == 0), stop=(j == NJ - 1)),
                         reads=[wd, act], writes=[pd], signal=(j == NJ - 1))
                s.op("dve", lambda e, m=m, pd=pd: e.scalar_tensor_tensor(
                    out=xt[:, m, :n], in0=pd[:, :n], scalar=g.modT[:, 5 * 8 + m, which:which + 1], in1=xt[:, m, :n],
                    op0=ALU.mult, op1=ALU.add), reads=[pd, xt, g.modT], writes=[xt])
            s.dma("sp", g.resid[:, t0:t0 + n].rearrange("(k p) n -> p k n", p=128), xt[:, :, :n], reads=[xt], writes=[g.resid])


def stage_dn_prep(g, l):
    nc, s = g.nc, g.s
    with Mem(g) as mem:
        cw = mem.sb("cw", [128, 6, 3], F32)
        dnp = mem.sb("dnp", [16, 2], F32)
        rowm = mem.sb("rowm", [16, 2], F32)
        xhs = [mem.sb("xh%d" % i, [128, 6, 514], F32) for i in range(2)]
        accs = [mem.sb("acc%d" % i, [128, 512], F32) for i in range(2)]
        sxs = [mem.sb("sx%d" % i, [128, 512], F32) for i in range(2)]
        sqs = [mem.sb("dsq%d" % i, [128, 512], BF16) for i in range(2)]
        rss = [mem.sb("drs%d" % i, [128, 512], F32) for i in range(2)]
        obs = [mem.sb("dob%d" % i, [128, 4, 512], F32) for i in range(2)]
        qbs = [mem.sb("dqb%d" % i, [128, 512], F32) for i in range(2)]
        tms = [mem.sb("dtm%d" % i, [128, 4, 512], F32) for i in range(2)]
        abs_ = [mem.sb("dab%d" % i, [16, 512], F32) for i in range(2)]
        g16s = [mem.sb("dg16%d" % i, [16, 512], F32) for i in range(2)]
        b16s = [mem.sb("db16%d" % i, [16, 512], F32) for i in range(2)]
        gbts = [mem.sb("dgbt%d" % i, [16, 512], F32) for i in range(2)]
        gtms = [mem.sb("dgtm%d" % i, [128, 4, 16], F32) for i in range(2)]
        pss = [mem.ps("dss%d" % i, [128, 512]) for i in range(2)]
        ptr = [mem.ps("dtr%d" % i, [128, 4, 128]) for i in range(2)]
        pgt = [mem.ps("dgt%d" % i, [128, 4, 16]) for i in range(2)]
        s.dma("sp", cw[:], g.dconvw_in[:, l, :, :], writes=[cw])
        s.dma("sp", dnp[:], g.dnp_in[:, l, :], writes=[dnp])
        s.dma("sp", rowm[:], g.drowmask_in[:, :], writes=[rowm])
        s.op("act", lambda e: e.activation(out=dnp[:, 1:2], in_=dnp[:, 1:2], func=AF.Exp), reads=[dnp], writes=[dnp])
        s.op("dve", lambda e: e.tensor_scalar(out=dnp[:, 1:2], in0=dnp[:, 1:2], scalar1=-1.0, scalar2=None, op0=ALU.mult),
             reads=[dnp], writes=[dnp])
        ic = 0
        for ti, (t0, n) in enumerate(TT):
            xh = xhs[ti % 2]
            ob = obs[ti % 2]
            tm = tms[ti % 2]
            seg0, seg1 = (0, L) if ti == 0 else (L, T)
            lo = t0 - 1 if t0 > seg0 else t0
            hi = t0 + n + 1 if t0 + n < seg1 else t0 + n
            if lo == t0:
                s.op("pool", lambda e, xh=xh: e.memset(xh[:, :, 0:1], 0.0), writes=[xh])
            if hi == t0 + n:
                s.op("pool", lambda e, xh=xh: e.memset(xh[:, :, n + 1:n + 2], 0.0), writes=[xh])
            off = 1 - (t0 - lo)
            s.dma("sp", xh[:, :, off:off + (hi - lo)], g.pT[0:768, lo:hi].rearrange("(c p) n -> p c n", p=128),
                  reads=[g.pT], writes=[xh])
            for c in range(6):
                acc, sx, sq, rs, ss = accs[ic % 2], sxs[ic % 2], sqs[ic % 2], rss[ic % 2], pss[ic % 2]
                qb = qbs[ic % 2]
                eng = "dve"
                ic += 1
                s.op("pool", lambda e, c=c, acc=acc: e.tensor_scalar(out=acc[:, :n], in0=xh[:, c, 1:n + 1], scalar1=cw[:, c, 1:2],
                                                                  scalar2=None, op0=ALU.mult), reads=[xh, cw], writes=[acc])
                s.op(eng, lambda e, c=c, acc=acc: e.scalar_tensor_tensor(out=acc[:, :n], in0=xh[:, c, 0:n], scalar=cw[:, c, 0:1],
                                                                         in1=acc[:, :n], op0=ALU.mult, op1=ALU.add),
                     reads=[xh, cw, acc], writes=[acc])
                s.op(eng, lambda e, c=c, acc=acc: e.scalar_tensor_tensor(out=acc[:, :n], in0=xh[:, c, 2:n + 2], scalar=cw[:, c, 2:3],
                                                                         in1=acc[:, :n], op0=ALU.mult, op1=ALU.add),
                     reads=[xh, cw, acc], writes=[acc])
                if c >= 4:
                    s.op("act", lambda e, c=c, acc=acc: e.activation(out=ob[:, c - 2, :n], in_=acc[:, :n], func=AF.Silu),
                         reads=[acc], writes=[ob])
                    continue
                s.op("act", lambda e, acc=acc, sx=sx: e.activation(out=sx[:, :n], in_=acc[:, :n], func=AF.Silu), reads=[acc], writes=[sx])
                s.op("act", lambda e, sx=sx, sq=sq: e.activation(out=sq[:, :n], in_=sx[:, :n], func=AF.Square), reads=[sx], writes=[sq])
                s.op("pe", lambda e, sq=sq, ss=ss: e.matmul(ss[:, :n], g.bdones[:, :], sq[:, :n], start=True, stop=True),
                     reads=[g.bdones, sq], writes=[ss])
                s.op("dve", lambda e, rs=rs, ss=ss: e.tensor_scalar(out=rs[:, :n], in0=ss[:, :n], scalar1=EPS, scalar2=None, op0=ALU.add),
                     reads=[ss], writes=[rs])
                s.op("act", lambda e, rs=rs: e.activation(out=rs[:, :n], in_=rs[:, :n], func=AF.Sqrt), reads=[rs], writes=[rs])
                s.op("dve", lambda e, rs=rs: e.reciprocal(out=rs[:, :n], in_=rs[:, :n]), reads=[rs], writes=[rs])
                if c < 2:
                    s.op("dve", lambda e, sx=sx, rs=rs, qb=qb: e.scalar_tensor_tensor(out=qb[:, :n], in0=sx[:, :n], scalar=0.125,
                                                                                     in1=rs[:, :n], op0=ALU.mult, op1=ALU.mult),
                         reads=[sx, rs], writes=[qb])
                    s.dma("sp", g.dqk[c, :, t0:t0 + n], qb[:, :n], reads=[qb], writes=[g.dqk])
                else:
                    s.op("dve", lambda e, c=c, sx=sx, rs=rs: e.tensor_tensor(out=ob[:, c - 2, :n], in0=sx[:, :n], in1=rs[:, :n], op=ALU.mult),
                         reads=[sx, rs], writes=[ob])
                    s.dma("sp", g.dqk[c, :, t0:t0 + n], ob[:, c - 2, :n], reads=[ob], writes=[g.dqk])
            nj = n // 128
            for j in range(nj):
                pt = ptr[j % 2]
                for q, slot in enumerate((2, 3, 0, 1)):
                    s.op("pe", lambda e, j=j, q=q, slot=slot, pt=pt: e.transpose(pt[:, q, :], ob[:, slot, j * 128:(j + 1) * 128], g.cf32[:, 0, :]),
                         reads=[ob, g.cf32], writes=[pt], signal=(q == 3))
                if j % 2 == 0:
                    s.op("act", lambda e, j=j, pt=pt: e.copy(out=tm[:, j, :].rearrange("p (a b) -> p a b", a=4), in_=pt[:, :, :]),
                         reads=[pt], writes=[tm])
                else:
                    s.op("dve", lambda e, j=j, pt=pt: e.tensor_copy(out=tm[:, j, :].rearrange("p (a b) -> p a b", a=4), in_=pt[:, :, :]),
                         reads=[pt], writes=[tm])
            s.dma("sp", g.dvk[t0:t0 + n, :].rearrange("(j p) c -> p j c", p=128), tm[:, :nj, :], reads=[tm], writes=[g.dvk])
            ab, g16, b16, gbt, gtm, pg = abs_[ti % 2], g16s[ti % 2], b16s[ti % 2], gbts[ti % 2], gtms[ti % 2], pgt[ti % 2]
            s.dma("sp", ab[:, :n], g.pT[8 * 128:8 * 128 + 16, t0:t0 + n], reads=[g.pT], writes=[ab])
            s.op("act", lambda e: e.activation(out=g16[:, :n], in_=ab[:, :n], func=AF.Exp, bias=dnp[:, 0:1]), reads=[ab, dnp], writes=[g16])
            s.op("act", lambda e: e.activation(out=g16[:, :n], in_=g16[:, :n], func=AF.Ln, bias=1.0), reads=[g16], writes=[g16])
            s.op("act", lambda e: e.activation(out=b16[:, :n], in_=ab[:, :n], func=AF.Sigmoid), reads=[ab], writes=[b16])
            s.op("dve", lambda e: e.tensor_scalar(out=g16[:, :n], in0=g16[:, :n], scalar1=dnp[:, 1:2], scalar2=rowm[:, 0:1],
                                                  op0=ALU.mult, op1=ALU.mult), reads=[g16, dnp, rowm], writes=[g16])
            s.op("dve", lambda e: e.scalar_tensor_tensor(out=gbt[:, :n], in0=b16[:, :n], scalar=rowm[:, 1:2], in1=g16[:, :n],
                                                         op0=ALU.mult, op1=ALU.add), reads=[b16, g16, rowm], writes=[gbt])
            s.dma("sp", g.dgbT[:, t0:t0 + n], gbt[:, :n], reads=[gbt], writes=[g.dgbT])
            for j in range(nj):
                s.op("pe", lambda e, j=j: e.transpose(pg[:, j, :], gbt[:, j * 128:(j + 1) * 128], g.cf32[0:16, 0, 0:16]),
                     reads=[gbt, g.cf32], writes=[pg], signal=(j == nj - 1))
            s.op("dve", lambda e: e.tensor_copy(out=gtm[:, :nj, :], in_=pg[:, :nj, :]), reads=[pg], writes=[gtm])
            s.dma("sp", g.dgb[t0:t0 + n, :].rearrange("(j p) c -> p j c", p=128), gtm[:, :nj, :], reads=[gtm], writes=[g.dgb])


def stage_dn_chain(g, l):
    nc, s = g.nc, g.s
    with_ctx = (l + g.layer0) < DEPTH - 1
    NTK = T // 128
    with Mem(g) as mem:
        gb = mem.sb("cgb", [128, NTK, 16], F32)
        mext = mem.sb("cmext", [128, 2, 130], F32)
        same = mem.sb("csame", [128, 128], F32)
        posm = mem.sb("cposm", [128, 2, 2, 128], F32)
        offd = mem.sb("coffd", [128, 4, 128], F32)
        sel4 = mem.sb("csel4", [4, 2, 128], F32)
        sel16 = mem.sb("csel16", [16, 2, 2, 128], F32)
        ident4 = mem.sb("cident4", [128, 4, 128], F32)
        Sf = mem.sb("cSf", [128, 4, 128], F32)
        vns = [mem.sb("cvn%d" % d, [128, 4, 64], F32) for d in range(2)]
        NB = 2
        def mk(name, shape, dt):
            return [[mem.sb("%s%d%d" % (name, d, i), shape, dt) for i in range(NB)] for d in range(2)]

        def mk1(name, shape, dt):
            one = [mem.sb("%s%d" % (name, d), shape, dt) for d in range(2)]
            return [[one[d]] * NB for d in range(2)]

        qkb_b = mk("cqkb", [128, 4, 128], F32)
        vkb_b = mk("cvkb", [128, 512], F32)
        eT_b = mk("ceT", [4, 130], F32)
        gbT_b = mk("cgbT", [16, 128], F32)
        sc_b = mk("csc", [128, 12], F32)
        esc_b = mk("cesc", [128, 8], F32)
        bg_b = mk("cbg", [128, 4], F32)
        GM_b = mk1("cGM", [128, 4, 128], F32)
        DT_b = mk1("cDT", [128, 4, 128], F32)
        Ds_b = mk1("cDs", [128, 4, 128], F32)
        DsT_b = mk1("cDsT", [128, 4, 128], F32)
        kbT_b = mk("ckbT", [128, 2, 128], F32)
        qdT_b = mk("cqdT", [128, 2, 128], F32)
        gend_b = mk("cgend", [128, 2, 2], F32)
        A_b = mk1("cA", [128, 4, 128], F32)
        At_b = mk1("cAt", [128, 4, 128], F32)
        iT_b = mk("ciT", [128, 4, 128], F32)
        P_b = [mk1("cP%d" % i, [128, 4, 128], F32) for i in range(2)]
        Pt_b = [mk1("cPt%d" % i, [128, 4, 128], F32) for i in range(2)]
        Mt_b = [mk1("cMt%d" % i, [128, 4, 128], F32) for i in range(2)]
        bv_b = mk("cbv", [128, 4, 64], F32)
        kbg_b = mk("ckbg", [128, 4, 64], F32)
        kdec_b = mk("ckdec", [128, 4, 64], F32)
        u_b = mk("cu", [128, 4, 64], F32)
        wT_b = mk("cwT", [128, 2, 128], F32)
        osb_b = mk("cosb", [128, 256], F32)
        banks = [mem.ps("cb%d" % i, [128, 512]) for i in range(8)]
        bi = [0]

        def bank():
            b = banks[bi[0] % 8]
            bi[0] += 1
            return b

        s.dma("sp", gb[:], g.dgb[:, :].rearrange("(j p) c -> p j c", p=128), reads=[g.dgb], writes=[gb])
        s.dma("sp", mext[:], g.dmaskext_in[:, :, :], writes=[mext])
        s.dma("sp", same[:], g.dsame_in[:, :], writes=[same])
        s.dma("sp", posm[:], g.dposmask_in[:, :, :, :], writes=[posm])
        s.dma("sp", offd[:], g.doffdiag_in[:, :, :], writes=[offd])
        s.dma("sp", sel4[:], g.dsel4_in[:, :, :], writes=[sel4])
        s.dma("sp", sel16[:], g.dsel16_in[:, :, :, :], writes=[sel16])
        for h in range(4):
            s.op("dve", lambda e, h=h: e.tensor_copy(out=ident4[:, h, :], in_=g.cf32[:, 0, :]), reads=[g.cf32], writes=[ident4])
        s.op("pool", lambda e: e.memset(Sf[:], 0.0), writes=[Sf])
        for d in range(2):
            s.op("pool", lambda e, d=d: e.memset(vns[d][:], 0.0), writes=[vns[d]])

        def flat(ap):
            return ap.rearrange("p a b -> p (a b)")

        def prep(d, j, bs):
            tok = slice(j * 128, (j + 1) * 128)
            gcols = gb[:, j, d * 4:(d + 1) * 4]
            bcols = gb[:, j, 8 + d * 4:8 + (d + 1) * 4]
            eT, sc, esc, bg, GM = eT_b[d][bs], sc_b[d][bs], esc_b[d][bs], bg_b[d][bs], GM_b[d][bs]
            DT, Ds, DsT = DT_b[d][bs], Ds_b[d][bs], DsT_b[d][bs]
            Dm = Ds
            kbT, qdT, gend = kbT_b[d][bs], qdT_b[d][bs], gend_b[d][bs]
            A, At, iT = A_b[d][bs], At_b[d][bs], iT_b[d][bs]
            bv, kbg, kdec, u, wT = bv_b[d][bs], kbg_b[d][bs], kdec_b[d][bs], u_b[d][bs], wT_b[d][bs]
            gbT = gbT_b[d][bs]
            qk, vkb = qkb_b[d][bs], vkb_b[d][bs]
            s.dma("sp", gbT[:, :], g.dgbT[:, tok], reads=[g.dgbT], writes=[gbT])
            s.dma("sp", qk[:, :, :], g.dqk[:, :, tok].rearrange("c p n -> p c n"), reads=[g.dqk], writes=[qk])
            s.dma("sp", vkb[:, :], g.dvk[tok, :], reads=[g.dvk], writes=[vkb])
            b1 = bank()
            s.op("pe", lambda e: e.matmul(b1[0:4, 0:130], gcols, mext[:, d, :], start=True, stop=True), reads=[gb, mext], writes=[b1])
            s.op("act", lambda e: e.activation(out=eT[:, :], in_=b1[0:4, 0:130], func=AF.Exp), reads=[b1], writes=[eT])
            if "dn_sec1" in g.debug:
                return
            b2 = bank()
            s.op("pe", lambda e: e.matmul(b2[:, 0:4], mext[:, d, 0:128], gcols, start=True, stop=True), reads=[gb, mext], writes=[b2],
                 signal=False)
            s.op("pe", lambda e: e.matmul(b2[:, 4:8], same[:, :], gcols, start=True, stop=True), reads=[gb, same], writes=[b2])
            s.op("dve", lambda e: e.tensor_copy(out=sc[:, 0:4], in_=b2[:, 0:4]), reads=[b2], writes=[sc])
            s.op("dve", lambda e: e.tensor_tensor(out=sc[:, 4:8], in0=b2[:, 4:8], in1=sc[:, 0:4], op=ALU.subtract), reads=[b2, sc], writes=[sc])
            s.op("dve", lambda e: e.tensor_scalar(out=sc[:, 8:12], in0=sc[:, 0:4], scalar1=-1.0, scalar2=None, op0=ALU.mult),
                 reads=[sc], writes=[sc])
            s.op("act", lambda e: e.activation(out=esc[:, :], in_=sc[:, 0:8], func=AF.Exp), reads=[sc], writes=[esc])
            s.op("dve", lambda e: e.tensor_tensor(out=bg[:, :], in0=bcols, in1=esc[:, 0:4], op=ALU.mult), reads=[gb, esc], writes=[bg])
            if "dn_sec2" in g.debug:
                return
            s.op("dve", lambda e: e.tensor_tensor(out=GM[:, :, :], in0=mext[:, d, 0:128].unsqueeze(1).broadcast_to([128, 4, 128]),
                                                  in1=gcols.unsqueeze(2).broadcast_to([128, 4, 128]), op=ALU.mult),
                 reads=[mext, gb], writes=[GM])
            ba, bb = bank(), bank()
            for (bk, a) in ((ba, 0), (bb, 1)):
                for h in range(4):
                    s.op("pe", lambda e, bk=bk, h=h: e.matmul(bk[:, h * 128:(h + 1) * 128], g.onesf[:, :], GM[:, h, :], start=True, stop=False),
                         reads=[g.onesf, GM], writes=[bk], signal=False)
                    s.op("pe", lambda e, bk=bk, a=a, h=h: e.matmul(bk[:, h * 128:(h + 1) * 128], g.cf32[:, 0, :], posm[:, d, a, :],
                                                                  start=False, stop=True),
                         reads=[g.cf32, posm], writes=[bk], signal=(h == 3))
            for h in range(4):
                s.op("act", lambda e, h=h: e.activation(out=Dm[:, h, :], in_=ba[:, h * 128:(h + 1) * 128], func=AF.Exp, scale=-1.0,
                                                        bias=sc[:, h:h + 1]), reads=[ba, sc], writes=[Dm])
                s.op("act", lambda e, h=h: e.activation(out=DT[:, h, :], in_=bb[:, h * 128:(h + 1) * 128], func=AF.Exp, scale=1.0,
                                                        bias=sc[:, 8 + h:9 + h]), reads=[bb, sc], writes=[DT])
            s.op("pool", lambda e: e.tensor_tensor(out=Ds[:, :, :], in0=Ds[:, :, :], in1=offd[:, :, :], op=ALU.mult), reads=[Ds, offd], writes=[Ds])
            s.op("pool", lambda e: e.tensor_tensor(out=DsT[:, :, :], in0=DT[:, :, :], in1=offd[:, :, :], op=ALU.mult), reads=[DT, offd], writes=[DsT])
            if "dn_sec3" in g.debug:
                return
            bE, bB = bank(), bank()
            for c in range(2):
                s.op("pe", lambda e, c=c: e.matmul(bE[:, c * 130:(c + 1) * 130], sel4[:, c, :], eT[:, :], start=True, stop=True),
                     reads=[sel4, eT], writes=[bE])
                s.op("pe", lambda e, c=c: e.matmul(bB[:, c * 128:(c + 1) * 128], sel16[:, d, c, :], gbT[:, :], start=True, stop=True),
                     reads=[sel16, gbT], writes=[bB])
            s.op("dve", lambda e: e.tensor_tensor(out=kbT[:, :, :], in0=qk[:, 2:4, :], in1=bB[:, 0:256].rearrange("p (c n) -> p c n", c=2),
                                                  op=ALU.mult), reads=[qk, bB], writes=[kbT])
            s.op("dve", lambda e: e.tensor_tensor(out=qdT[:, :, :], in0=qk[:, 0:2, :],
                                                  in1=bE[:, 0:260].rearrange("p (c n) -> p c n", c=2)[:, :, 0:128], op=ALU.mult),
                 reads=[qk, bE], writes=[qdT])
            s.op("act", lambda e: e.copy(out=gend[:, :, :], in_=bE[:, 0:260].rearrange("p (c n) -> p c n", c=2)[:, :, 128:130]),
                 reads=[bE], writes=[gend])
            if "dn_sec4" in g.debug:
                return
            pAe, pAo, pAte, pAto, pQe, pQo = bank(), bank(), bank(), bank(), bank(), bank()
            for hh, (bA, bAt, bQ) in enumerate(((pAe, pAte, pQe), (pAo, pAto, pQo))):
                r0 = hh * 64
                for c in range(2):
                    cs = slice(c * 128, (c + 1) * 128)
                    s.op("pe", lambda e, c=c, r0=r0, bA=bA, cs=cs: e.matmul(bA[:, cs], kbT[r0:r0 + 64, c, :], qk[r0:r0 + 64, 2 + c, :],
                                                                           start=True, stop=True), reads=[kbT, qk], writes=[bA], signal=False)
                    s.op("pe", lambda e, c=c, r0=r0, bAt=bAt, cs=cs: e.matmul(bAt[:, cs], qk[r0:r0 + 64, 2 + c, :], kbT[r0:r0 + 64, c, :],
                                                                             start=True, stop=True), reads=[kbT, qk], writes=[bAt], signal=False)
                    s.op("pe", lambda e, c=c, r0=r0, bQ=bQ, cs=cs: e.matmul(bQ[:, cs], qk[r0:r0 + 64, 2 + c, :], qk[r0:r0 + 64, c, :],
                                                                           start=True, stop=True), reads=[qk], writes=[bQ], signal=(c == 1))

            def hv(buf, hh):
                return buf[:, :, :].rearrange("p (c x) n -> p c x n", x=2)[:, :, hh, :]

            for hh, (bA, bAt, bQ) in enumerate(((pAe, pAte, pQe), (pAo, pAto, pQo))):
                s.op("dve", lambda e, hh=hh, bA=bA: e.tensor_tensor(out=hv(A, hh), in0=bA[:, 0:256].rearrange("p (c n) -> p c n", c=2),
                                                                   in1=hv(Ds, hh), op=ALU.mult), reads=[bA, Ds], writes=[A])
                s.op("dve", lambda e, hh=hh, bAt=bAt: e.tensor_tensor(out=hv(At, hh), in0=bAt[:, 0:256].rearrange("p (c n) -> p c n", c=2),
                                                                     in1=hv(DsT, hh), op=ALU.mult), reads=[bAt, DsT], writes=[At])
                s.op("dve", lambda e, hh=hh, bQ=bQ: e.tensor_tensor(out=hv(iT, hh), in0=bQ[:, 0:256].rearrange("p (c n) -> p c n", c=2),
                                                                   in1=hv(DT, hh), op=ALU.mult), reads=[bQ, DT], writes=[iT])
            if "dn_sec5" in g.debug:
                return
            Mt = Mt_b[0][d][bs]
            s.op("pool", lambda e: e.tensor_tensor(out=Mt[:, :, :], in0=ident4[:, :, :], in1=At[:, :, :], op=ALU.subtract),
                 reads=[ident4, At], writes=[Mt])
            P, Pt = A, At
            for lvl in range(1, 6):
                Pn, Ptn, Mtn = P_b[lvl % 2][d][bs], Pt_b[lvl % 2][d][bs], Mt_b[lvl % 2][d][bs]
                pP = bank()
                for h in range(4):
                    s.op("pe", lambda e, h=h: e.matmul(pP[:, h * 128:(h + 1) * 128], Pt[:, h, :], P[:, h, :], start=True, stop=True),
                         reads=[P, Pt], writes=[pP], signal=(h == 3))
                s.op("act", lambda e: e.copy(out=flat(Pn[:, :, :]), in_=pP[:, :]), reads=[pP], writes=[Pn])
                if lvl < 5:
                    pPt = bank()
                    for h in range(4):
                        s.op("pe", lambda e, h=h: e.matmul(pPt[:, h * 128:(h + 1) * 128], P[:, h, :], Pt[:, h, :], start=True, stop=True),
                             reads=[P, Pt], writes=[pPt], signal=(h == 3))
                    s.op("dve", lambda e: e.tensor_copy(out=flat(Ptn[:, :, :]), in_=pPt[:, :]), reads=[pPt], writes=[Ptn])
                pM = bank()
                for h in range(4):
                    s.op("pe", lambda e, h=h: e.matmul(pM[:, h * 128:(h + 1) * 128], Pn[:, h, :], Mt[:, h, :], start=True, stop=True),
                         reads=[Pn, Mt], writes=[pM], signal=(h == 3))
                s.op("dve", lambda e: e.tensor_tensor(out=flat(Mtn[:, :, :]), in0=pM[:, :], in1=flat(Mt[:, :, :]), op=ALU.add),
                     reads=[pM, Mt], writes=[Mtn])
                P, Pt, Mt = Pn, Ptn, Mtn
            Tt = Mt
            if "dn_sec6" in g.debug:
                return
            s.op("pool", lambda e: e.tensor_tensor(out=bv[:, :, :], in0=vkb[:, 0:256].rearrange("p (h x) -> p h x", h=4),
                                                   in1=bcols.unsqueeze(2).broadcast_to([128, 4, 64]), op=ALU.mult), reads=[vkb, gb], writes=[bv])
            s.op("pool", lambda e: e.tensor_tensor(out=kbg[:, :, :], in0=vkb[:, 256:512].rearrange("p (h x) -> p h x", h=4),
                                                   in1=bg[:, :].unsqueeze(2).broadcast_to([128, 4, 64]), op=ALU.mult), reads=[vkb, bg], writes=[kbg])
            s.op("pool", lambda e: e.tensor_tensor(out=kdec[:, :, :], in0=vkb[:, 256:512].rearrange("p (h x) -> p h x", h=4),
                                                   in1=esc[:, 4:8].unsqueeze(2).broadcast_to([128, 4, 64]), op=ALU.mult), reads=[vkb, esc], writes=[kdec])
            pU = bank()
            for h in range(4):
                s.op("pe", lambda e, h=h: e.matmul(pU[:, h * 64:(h + 1) * 64], Tt[:, h, :], bv[:, h, :], start=True, stop=True),
                     reads=[Tt, bv], writes=[pU], signal=(h == 3))
            s.op("act", lambda e: e.copy(out=u[:, :, :].rearrange("p h x -> p (h x)"), in_=pU[:, 0:256]), reads=[pU], writes=[u])
            pW = bank()
            for h in range(4):
                c, r0 = h // 2, (h % 2) * 64
                s.op("pe", lambda e, h=h, c=c, r0=r0: e.matmul(pW[r0:r0 + 64, c * 128:(c + 1) * 128], kbg[:, h, :], Tt[:, h, :],
                                                              start=True, stop=True), reads=[kbg, Tt], writes=[pW], signal=(h == 3))
            s.op("dve", lambda e: e.tensor_copy(out=flat(wT[:, :, :]), in_=pW[:, 0:256]), reads=[pW], writes=[wT])

        def seq(d, j, bs, want_o):
            vn = vns[d]
            wT, qdT, iT, kdec, u, gend, osb = wT_b[d][bs], qdT_b[d][bs], iT_b[d][bs], kdec_b[d][bs], u_b[d][bs], gend_b[d][bs], osb_b[d][bs]
            pO = bank()
            for e_ in ((0, 1) if d == 0 else (1, 0)):
                er = slice(e_ * 64, (e_ + 1) * 64)
                pS = bank()
                for c in range(2):
                    dc = d * 2 + c
                    s.op("pe", lambda e, c=c, dc=dc: e.matmul(pS[er, c * 128:(c + 1) * 128], wT[:, c, er], Sf[:, dc, :], start=True, stop=True),
                         reads=[wT, Sf], writes=[pS], signal=(c == 1))
                s.op("dve", lambda e: e.tensor_tensor(out=vn[er, :, :].rearrange("p h x -> p (h x)"), in0=u[er, :, :].rearrange("p h x -> p (h x)"),
                                                      in1=pS[er, 0:256], op=ALU.subtract), reads=[u, pS], writes=[vn])
                pD = bank()
                for c in range(2):
                    dc = d * 2 + c
                    for h in (2 * c, 2 * c + 1):
                        r0 = (h % 2) * 64
                        s.op("pe", lambda e, c=c, dc=dc, h=h, r0=r0: e.matmul(pO[er, h * 64:(h + 1) * 64], qdT[:, c, er], Sf[:, dc, r0:r0 + 64],
                                                                             start=True, stop=False),
                             reads=[qdT, Sf], writes=[pO], signal=False)
                        s.op("pe", lambda e, h=h: e.matmul(pO[er, h * 64:(h + 1) * 64], iT[:, h, er], vn[:, h, :], start=False, stop=True),
                             reads=[iT, vn], writes=[pO], signal=False)
                    s.op("pe", lambda e, c=c: e.matmul(pD[:, c * 128:(c + 1) * 128], kdec[er, 2 * c:2 * c + 2, :].rearrange("p h x -> p (h x)"),
                                                       vn[er, 2 * c:2 * c + 2, :].rearrange("p h x -> p (h x)"), start=True, stop=True),
                         reads=[kdec, vn], writes=[pD])
                for c in range(2):
                    dc = d * 2 + c
                    for hh in range(2):
                        r0 = hh * 64
                        s.op("dve", lambda e, c=c, dc=dc, r0=r0: e.scalar_tensor_tensor(
                            out=Sf[r0:r0 + 64, dc, r0:r0 + 64], in0=Sf[r0:r0 + 64, dc, r0:r0 + 64], scalar=gend[r0:r0 + 64, c, e_:e_ + 1],
                            in1=pD[r0:r0 + 64, c * 128 + r0:c * 128 + r0 + 64], op0=ALU.mult, op1=ALU.add),
                            reads=[Sf, gend, pD], writes=[Sf])
            if want_o:
                s.op("act", lambda e: e.copy(out=osb[:, :], in_=pO[:, 0:256]), reads=[pO], writes=[osb])
                s.dma("sp", g.do[d, j * 128:(j + 1) * 128, :], osb[:, :], reads=[osb], writes=[g.do])

        order = [list(range(NTK)), [1, 0] + list(range(NTK - 1, 1, -1))]
        nsteps = NTK if "dn_steps" not in g.debug else 3
        for d in range(2):
            prep(d, order[d][0], 0)
        for k in range(nsteps):
            if k + 1 < nsteps:
                for d in range(2):
                    prep(d, order[d][k + 1], (k + 1) % NB)
            if "dn_noseq" in g.debug:
                continue
            for d in range(2):
                j = order[d][k]
                seq(d, j, k % NB, with_ctx or j >= 2)


def stage_dn_out(g, l):
    nc, s = g.nc, g.s
    with_ctx = (l + g.layer0) < DEPTH - 1
    with Mem(g) as mem:
        ngf = mem.sb("eng", [128, DEPTH], F32)
        o0s = [mem.sb("eo0%d" % i, [128, 4, 256], F32) for i in range(2)]
        o1s = [mem.sb("eo1%d" % i, [128, 4, 256], F32) for i in range(2)]
        zs = [mem.sb("ez%d" % i, [128, 2, 512], F32) for i in range(2)]
        oTs = [mem.sb("eoT%d" % i, [128, 512], F32) for i in range(2)]
        sqs = [mem.sb("esq%d" % i, [128, 512], BF16) for i in range(2)]
        rss = [mem.sb("ers%d" % i, [128, 512], F32) for i in range(2)]
        ys = [mem.sb("ey%d" % i, [128, 512], BF16) for i in range(2)]
        pts = [mem.ps("ept%d" % i, [128, 512]) for i in range(2)]
        pss = [mem.ps("eps%d" % i, [128, 512]) for i in range(2)]
        s.dma("sp", ngf[:, 0:g.depth], g.dnormg_in[:, :], writes=[ngf])
        ic = 0
        for ti, (t0, n) in enumerate(TT):
            if ti == 0 and not with_ctx:
                continue
            nj = n // 128
            o0, o1, z = o0s[ti % 2], o1s[ti % 2], zs[ti % 2]
            s.dma("sp", o0[:, :nj, :], g.do[0, t0:t0 + n, :].rearrange("(j p) c -> p j c", p=128), reads=[g.do], writes=[o0])
            s.dma("sp", o1[:, :nj, :], g.do[1, t0:t0 + n, :].rearrange("(j p) c -> p j c", p=128), reads=[g.do], writes=[o1])
            s.dma("sp", z[:, :, :n], g.pT[6 * 128:8 * 128, t0:t0 + n].rearrange("(c p) n -> p c n", p=128), reads=[g.pT], writes=[z])
            s.op("pool", lambda e: e.tensor_tensor(out=o0[:, :nj, :], in0=o0[:, :nj, :], in1=o1[:, :nj, :], op=ALU.add), reads=[o0, o1], writes=[o0])
            s.op("act", lambda e: e.activation(out=z[:, :, :n], in_=z[:, :, :n], func=AF.Silu), reads=[z], writes=[z])
            for c in range(2):
                pt, ps_, oT, sq, rs, y = pts[ic % 2], pss[ic % 2], oTs[ic % 2], sqs[ic % 2], rss[ic % 2], ys[ic % 2]
                ic += 1
                for j in range(nj):
                    s.op("pe", lambda e, j=j, c=c, pt=pt: e.transpose(pt[:, j * 128:(j + 1) * 128], o0[:, j, c * 128:(c + 1) * 128], g.cf32[:, 0, :]),
                         reads=[o0, g.cf32], writes=[pt], signal=(j == nj - 1))
                s.op("act", lambda e, pt=pt, oT=oT: e.copy(out=oT[:, :n], in_=pt[:, :n]), reads=[pt], writes=[oT])
                s.op("act", lambda e, sq=sq, oT=oT: e.activation(out=sq[:, :n], in_=oT[:, :n], func=AF.Square), reads=[oT], writes=[sq])
                s.op("pe", lambda e, sq=sq, ps_=ps_: e.matmul(ps_[:, :n], g.bdones[:, :], sq[:, :n], start=True, stop=True),
                     reads=[g.bdones, sq], writes=[ps_])
                s.op("dve", lambda e, rs=rs, ps_=ps_: e.tensor_scalar(out=rs[:, :n], in0=ps_[:, :n], scalar1=1.0 / HD, scalar2=EPS,
                                                                      op0=ALU.mult, op1=ALU.add), reads=[ps_], writes=[rs])
                s.op("act", lambda e, rs=rs: e.activation(out=rs[:, :n], in_=rs[:, :n], func=AF.Sqrt), reads=[rs], writes=[rs])
                s.op("dve", lambda e, rs=rs: e.reciprocal(out=rs[:, :n], in_=rs[:, :n]), reads=[rs], writes=[rs])
                s.op("dve", lambda e, oT=oT, rs=rs: e.scalar_tensor_tensor(out=oT[:, :n], in0=oT[:, :n], scalar=ngf[:, l:l + 1], in1=rs[:, :n],
                                                                           op0=ALU.mult, op1=ALU.mult), reads=[oT, rs, ngf], writes=[oT])
                s.op("pool", lambda e, oT=oT, y=y, c=c: e.tensor_tensor(out=y[:, :n], in0=oT[:, :n], in1=z[:, c, :n], op=ALU.mult),
                     reads=[oT, z], writes=[y])
                s.dma("sp", g.yT[c * 128:(c + 1) * 128, t0:t0 + n], y[:, :n], reads=[y], writes=[g.yT])


def _w_in_perm():
    cols = []
    dn = 0
    cols += list(range(dn, dn + 1024))
    cols += list(range(1024, 1040)) + [-1] * 112
    for base in (1040, 1552):
        q = [list(range(base + h * 64, base + (h + 1) * 64)) for h in range(4)]
        cols += q[0] + q[2] + q[1] + q[3]
        cols += list(range(base + 256, base + 512))
    cols += list(range(2064, 2320))
    assert len(cols) == NP
    return np.array(cols)


_CONSTS = {}


def _host_consts():
    if _CONSTS:
        return _CONSTS
    f32 = np.float32
    rows = S // 64
    row = np.repeat(np.arange(rows, dtype=f32), 64)
    col = np.tile(np.arange(64, dtype=f32), rows)
    inv_freq = (10000.0 ** (-np.arange(16, dtype=f32) / 16)).astype(f32)
    ang = np.concatenate([row[:, None] * inv_freq, col[:, None] * inv_freq], -1).astype(f32)
    cs = np.stack([np.cos(ang).T, np.sin(ang).T], 0).astype(f32)
    idx = np.arange(128) % 32
    _CONSTS["ropeCS"] = np.ascontiguousarray(cs[:, idx, :].transpose(1, 0, 2))
    cf = np.zeros((128, 2, 128), f32)
    cf[:, 0, :] = np.eye(128, dtype=f32)
    for i in range(128):
        if i % 64 < 32:
            cf[i + 32, 1, i] = -1.0
        else:
            cf[i - 32, 1, i] = 1.0
    _CONSTS["cf32"] = cf
    kk = np.arange(128)[:, None, None]
    r = np.arange(6)[None, :, None]
    qq = np.arange(512)[None, None, :]
    wm = (np.abs(qq - (128 * (r - 1) + kk)) <= 128)
    _CONSTS["wmask"] = wm.astype(f32).astype(ml_dtypes.bfloat16)
    bf = ml_dtypes.bfloat16
    ss = np.arange(S, dtype=np.int64)
    ph = (np.outer(ss, ss) % S).astype(np.float64) * (2.0 * np.pi / S)
    dft = np.stack([np.cos(ph) / 512.0, -np.sin(ph) / 512.0], 0).astype(f32)
    dft = dft.reshape(2, 32, 128, 8, 512).transpose(3, 0, 2, 1, 4)
    _CONSTS["dft"] = np.ascontiguousarray(dft)
    sc = np.arange(L, dtype=np.int64)
    phc = (np.outer(sc, sc) % L).astype(np.float64) * (2.0 * np.pi / L)
    dftc = np.stack([np.cos(phc) / 128.0, -np.sin(phc) / 128.0], 0).astype(f32)
    _CONSTS["dftc"] = np.ascontiguousarray(dftc.reshape(2, 2, 128, L).transpose(0, 2, 1, 3))
    cc = np.arange(64, dtype=np.int64)
    phd = (np.outer(cc, cc) % 64).astype(np.float64) * (2.0 * np.pi / 64)
    bd = np.zeros((128, 2, 128), f32)
    for gi in range(2):
        bd[gi * 64:(gi + 1) * 64, 0, gi * 64:(gi + 1) * 64] = np.cos(phd)
        bd[gi * 64:(gi + 1) * 64, 1, gi * 64:(gi + 1) * 64] = np.sin(phd)
    _CONSTS["bdcs"] = bd
    ii = np.arange(128)
    same = (ii[:, None] // 64) == (ii[None, :] // 64)
    mext = np.zeros((128, 2, 130), f32)
    mext[:, 0, :128] = same & (ii[:, None] <= ii[None, :])
    mext[:, 1, :128] = same & (ii[:, None] >= ii[None, :])
    for d in range(2):
        mext[:, d, 128] = ii < 64
        mext[:, d, 129] = ii >= 64
    _CONSTS["dmaskext"] = mext
    _CONSTS["dsame"] = same.astype(f32)
    BIG = 60000.0
    valid = [same & (ii[:, None] >= ii[None, :]), same & (ii[:, None] <= ii[None, :])]
    posm = np.zeros((128, 2, 2, 128), f32)
    for d in range(2):
        posm[:, d, 0, :] = np.where(valid[d], 0.0, BIG)
        posm[:, d, 1, :] = np.where(valid[d].T, 0.0, -BIG)
    _CONSTS["dposmask"] = posm
    _CONSTS["doffdiag"] = np.ascontiguousarray(np.broadcast_to((1.0 - np.eye(128, dtype=f32))[:, None, :], (128, 4, 128)))
    sel4 = np.zeros((4, 2, 128), f32)
    sel16 = np.zeros((16, 2, 2, 128), f32)
    for c in range(2):
        for p in range(128):
            sel4[2 * c + p // 64, c, p] = 1.0
            for d in range(2):
                sel16[8 + d * 4 + 2 * c + p // 64, d, c, p] = 1.0
    _CONSTS["dsel4"] = sel4
    _CONSTS["dsel16"] = sel16
    rowm = np.zeros((16, 2), f32)
    rowm[:8, 0] = 1.0
    rowm[8:, 1] = 1.0
    _CONSTS["drowmask"] = rowm
    return _CONSTS


def _prep_inputs(inputs, depth):
    f32 = np.float32
    x = np.asarray(inputs["x"], f32)
    B = x.shape[0]
    perm = _w_in_perm()
    w_in = np.asarray(inputs["w_in"], f32)[:depth]
    w_in_p = np.zeros((depth, D, NP), f32)
    valid = perm >= 0
    w_in_p[:, :, valid] = w_in[:, :, perm[valid]]
    gains = np.stack([np.asarray(inputs["norm1_g"], f32)[:depth], np.asarray(inputs["norm2_g"], f32)[:depth]], 1)
    gains = np.ascontiguousarray(gains.reshape(depth, 2, 8, 128).transpose(3, 0, 1, 2))
    b_adaT = np.ascontiguousarray(np.asarray(inputs["b_ada"], f32)[:depth].reshape(depth, 48, 128).transpose(2, 0, 1))
    w_ada = np.ascontiguousarray(np.asarray(inputs["w_ada"], f32)[:depth])
    c = np.asarray(inputs["c"], f32)
    c_ctx = np.asarray(inputs["c_ctx"], f32)
    ctx = np.asarray(inputs["ctx"], f32)
    consts = _host_consts()
    w_out = np.ascontiguousarray(np.asarray(inputs["w_out"], f32)[:depth])
    w_gate = np.ascontiguousarray(np.asarray(inputs["w_ffn_gate"], f32)[:depth])
    w_up = np.ascontiguousarray(np.asarray(inputs["w_ffn_up"], f32)[:depth])
    w_down = np.ascontiguousarray(np.asarray(inputs["w_ffn_down"], f32)[:depth])
    hg = np.stack([np.asarray(inputs[k], f32)[:depth] for k in ("ga_q_norm", "ga_k_norm", "wa_q_norm", "wa_k_norm")], -1)
    hg = np.ascontiguousarray(np.concatenate([hg, hg], 1).transpose(1, 0, 2))
    sinkbc = np.ascontiguousarray(np.broadcast_to(np.asarray(inputs["wa_sink"], f32)[:depth].reshape(1, -1), (128, depth * 4)))
    dnp = np.zeros((16, depth, 2), f32)
    dnp[:8, :, 0] = np.asarray(inputs["dn_dt_bias"], f32)[:depth].reshape(depth, 8).T
    dnp[:8, :, 1] = np.asarray(inputs["dn_A_log"], f32)[:depth].reshape(depth, 8).T
    cwv = np.asarray(inputs["dn_conv_w"], f32)[:depth]
    dconvw = np.ascontiguousarray(cwv.reshape(depth, 3, 6, 128).transpose(3, 0, 2, 1))
    ng = np.asarray(inputs["dn_norm_g"], f32)[:depth]
    dnormg = np.ascontiguousarray(np.concatenate([ng, ng], 1).T)
    maps = []
    for b in range(B):
        cT = np.stack([c[b].reshape(8, 128).T, c_ctx.reshape(8, 128).T], -1)
        maps.append({
            "xT": np.ascontiguousarray(x[b].T),
            "ctxT": np.ascontiguousarray(ctx[b].T),
            "cT": np.ascontiguousarray(cT),
            "gains": gains,
            "w_ada": w_ada,
            "b_adaT": b_adaT,
            "w_in": w_in_p,
            "hgains": hg,
            "dnp": dnp, "dconvw": dconvw, "dnormg": dnormg,
            "w_out": w_out, "w_gate": w_gate, "w_up": w_up, "w_down": w_down,
            "sinkbc": sinkbc,
            **consts,
        })
    return maps


FUSED = True


def kernel(**inputs):
    if FUSED:
        maps = _prep_inputs(inputs, DEPTH)
        nc = build_nc(DEPTH)
        res = run_bass_kernel_spmd(nc, maps, core_ids=list(range(len(maps))))
        out = np.stack([np.ascontiguousarray(r["outT"].T) for r in res.results], 0)
        return out.astype(np.float32)
    per_layer_keys = ("norm1_g", "norm2_g", "w_ada", "b_ada", "w_in", "dn_conv_w", "dn_A_log", "dn_dt_bias", "dn_norm_g",
                      "ga_q_norm", "ga_k_norm", "wa_q_norm", "wa_k_norm", "wa_sink", "w_out", "w_ffn_gate", "w_ffn_up",
                      "w_ffn_down")
    xT = None
    for l in range(DEPTH):
        sub = dict(inputs)
        for k in per_layer_keys:
            sub[k] = np.asarray(inputs[k])[l:l + 1]
        maps = _prep_inputs(sub, 1)
        if xT is not None:
            for b in range(len(maps)):
                maps[b]["xT"] = xT[b]
                maps[b]["ctxT"] = cT[b]
        nc = build_nc(1, layer0=l)
        res = run_bass_kernel_spmd(nc, maps, core_ids=list(range(len(maps))))
        xT = [r["outT"] for r in res.results]
        if l < DEPTH - 1:
            cT = [r["ctxoutT"] for r in res.results]
    out = np.stack([np.ascontiguousarray(x.T) for x in xT], 0)
    return out.astype(np.float32)
```
